# Optimizing a Trainium2 kernel written in Bass

```python
import jax, jax.numpy as jnp
from jax import lax
import numpy as np

D_MODEL = 1024
BATCH = 4
SEQ = 4096
DEPTH = 4

GRID_W = 64
CTX_LEN = 256
HEAD_DIM = 64
ATTN_DIM = D_MODEL // 2
N_Q_HEADS = ATTN_DIM // HEAD_DIM
N_KV_HEADS = N_Q_HEADS // 4
Q_PER_KV = N_Q_HEADS // N_KV_HEADS
KV_DIM = N_KV_HEADS * HEAD_DIM
GMLP_DIM = D_MODEL - ATTN_DIM
GMLP_GROUP_DIM = 128
N_GMLP_GROUPS = GMLP_DIM // GMLP_GROUP_DIM
CHUNK = 128
MIX_DIM = ATTN_DIM + GMLP_DIM
IN_DIM = ATTN_DIM + 2 * KV_DIM + 2 * GMLP_DIM
FF_DIM = 4 * D_MODEL
Q_BLOCK = 128
ROPE_THETA = 10000.0
EPS = 1e-6
N_MOD = 6

kernel_name = "hybrid_gmlp_gqa_prefix_dit"


def rmsnorm(x, g):
    xf = x.astype(jnp.float32)
    y = xf * lax.rsqrt(jnp.mean(xf * xf, axis=-1, keepdims=True) + EPS)
    return (y * g.astype(jnp.float32)).astype(x.dtype)


def group_layernorm(v, g):
    vf = v.astype(jnp.float32)
    mu = jnp.mean(vf, axis=-1, keepdims=True)
    var = jnp.mean(jnp.square(vf - mu), axis=-1, keepdims=True)
    return ((vf - mu) * lax.rsqrt(var + EPS) * g.astype(jnp.float32)).astype(v.dtype)


def modulation(cond, w_mod, b_mod):
    m = jax.nn.silu(cond) @ w_mod + b_mod
    return m.reshape(m.shape[:-1] + (N_MOD, D_MODEL))


def modulate(h, shift, scale):
    return h * (1 + scale) + shift


def axial_rope_tables(n_lat):
    rows = n_lat // GRID_W
    row = jnp.repeat(jnp.arange(rows, dtype=jnp.float32), GRID_W)
    col = jnp.tile(jnp.arange(GRID_W, dtype=jnp.float32), rows)
    n_freq = HEAD_DIM // 4
    inv_freq = ROPE_THETA ** (-jnp.arange(n_freq, dtype=jnp.float32) / n_freq)
    ang = jnp.stack([row[:, None] * inv_freq, col[:, None] * inv_freq], axis=1)
    return jnp.cos(ang), jnp.sin(ang)


def apply_rope(x, cos, sin):
    xf = x.astype(jnp.float32).reshape(x.shape[:-1] + (2, 2, HEAD_DIM // 4))
    x1, x2 = xf[..., 0, :], xf[..., 1, :]
    out = jnp.stack([x1 * cos - x2 * sin, x2 * cos + x1 * sin], axis=-2)
    return out.reshape(x.shape).astype(x.dtype)


def q_heads(q):
    b, n, _ = q.shape
    return q.reshape(b, n, N_KV_HEADS, Q_PER_KV, HEAD_DIM).transpose(0, 2, 3, 1, 4)


def kv_heads(k):
    b, n, _ = k.shape
    return k.reshape(b, n, N_KV_HEADS, HEAD_DIM).transpose(0, 2, 1, 3)


def gqa_blocks(q, k, v):
    b, kvh, g, n, dh = q.shape
    nb = n // Q_BLOCK
    qb = jnp.moveaxis(q.reshape(b, kvh, g, nb, Q_BLOCK, dh), 3, 0)

    def one_block(qblk):
        s = jnp.einsum('bkgqd,bkmd->bkgqm', qblk, k).astype(jnp.float32)
        p = jax.nn.softmax(s, axis=-1).astype(v.dtype)
        return jnp.einsum('bkgqm,bkmd->bkgqd', p, v)

    o = lax.map(one_block, qb)
    o = jnp.moveaxis(o, 0, 3).reshape(b, kvh, g, n, dh)
    return o.transpose(0, 3, 1, 2, 4).reshape(b, n, ATTN_DIM)


def chunk_gmlp(u, vg, w_s, b_s, g_norm):
    b, n, _ = u.shape
    u = jax.nn.gelu(u)
    vg = group_layernorm(jax.nn.gelu(vg).reshape(b, n, N_GMLP_GROUPS, GMLP_GROUP_DIM),
                         g_norm.reshape(N_GMLP_GROUPS, GMLP_GROUP_DIM))
    vc = vg.reshape(b, n // CHUNK, CHUNK, N_GMLP_GROUPS, GMLP_GROUP_DIM)
    mixed = jnp.einsum('gpq,bnqgc->bnpgc', w_s, vc) + b_s.T[None, None, :, :, None]
    return u * mixed.reshape(b, n, GMLP_DIM)


def sqrelu_mlp(h, w1, w2):
    return jnp.square(jax.nn.relu(h @ w1)) @ w2


def setup_inputs(seed: int = 0) -> dict:
    key = jax.random.key(seed)
    ks = jax.random.split(key, 17)
    f32 = jnp.float32
    nrm = lambda k, shape, s: (jax.random.normal(k, shape, f32) * s)
    gain = lambda k, shape: 1.0 + 0.02 * jax.random.normal(k, shape, f32)
    return {
        "x": nrm(ks[0], (BATCH, SEQ, D_MODEL), 1.0),
        "c": nrm(ks[1], (BATCH, D_MODEL), 1.0),
        "ctx": nrm(ks[2], (BATCH, CTX_LEN, D_MODEL), 1.0),
        "c_ctx": nrm(ks[3], (D_MODEL,), 1.0),
        "w_mod": nrm(ks[4], (DEPTH, D_MODEL, N_MOD * D_MODEL), D_MODEL ** -0.5),
        "b_mod": nrm(ks[5], (DEPTH, N_MOD * D_MODEL), 0.02),
        "norm1_g": gain(ks[6], (DEPTH, D_MODEL)),
        "w_in": nrm(ks[7], (DEPTH, D_MODEL, IN_DIM), D_MODEL ** -0.5),
        "q_norm_g": gain(ks[8], (DEPTH, HEAD_DIM)),
        "k_norm_g": gain(ks[9], (DEPTH, HEAD_DIM)),
        "gmlp_norm_g": gain(ks[10], (DEPTH, GMLP_DIM)),
        "w_spatial": nrm(ks[11], (DEPTH, N_GMLP_GROUPS, CHUNK, CHUNK), CHUNK ** -0.5),
        "b_spatial": gain(ks[12], (DEPTH, N_GMLP_GROUPS, CHUNK)),
        "w_out": nrm(ks[13], (DEPTH, MIX_DIM, D_MODEL), MIX_DIM ** -0.5),
        "norm2_g": gain(ks[14], (DEPTH, D_MODEL)),
        "w_ff1": nrm(ks[15], (DEPTH, D_MODEL, FF_DIM), D_MODEL ** -0.5),
        "w_ff2": nrm(ks[16], (DEPTH, FF_DIM, D_MODEL), FF_DIM ** -0.5),
    }


def reference(x, c, ctx, c_ctx, w_mod, b_mod, norm1_g, w_in, q_norm_g, k_norm_g, gmlp_norm_g,
              w_spatial, b_spatial, w_out, norm2_g, w_ff1, w_ff2):
    n_lat = x.shape[1]
    cos, sin = axial_rope_tables(n_lat)
    q_scale = HEAD_DIM ** -0.5
    kv_lo, kv_hi = ATTN_DIM, ATTN_DIM + 2 * KV_DIM
    x_lat, x_ctx = x, ctx
    for l in range(DEPTH):
        last = l == DEPTH - 1
        m_lat = modulation(c, w_mod[l], b_mod[l])
        m_ctx = modulation(c_ctx, w_mod[l], b_mod[l])
        sh1, sc1, ga1, sh2, sc2, ga2 = [m_lat[:, i, None, :] for i in range(N_MOD)]
        csh1, csc1, cga1, csh2, csc2, cga2 = [m_ctx[None, i, None, :] for i in range(N_MOD)]

        h_lat = modulate(rmsnorm(x_lat, norm1_g[l]), sh1, sc1)
        h_ctx = modulate(rmsnorm(x_ctx, norm1_g[l]), csh1, csc1)
        z_lat = h_lat @ w_in[l]
        q_l = z_lat[..., :ATTN_DIM]
        k_l = z_lat[..., kv_lo:kv_lo + KV_DIM]
        v_l = z_lat[..., kv_lo + KV_DIM:kv_hi]
        u_l = z_lat[..., kv_hi:kv_hi + GMLP_DIM]
        g_l = z_lat[..., kv_hi + GMLP_DIM:]
        if last:
            z_ctx = h_ctx @ w_in[l][:, kv_lo:kv_hi]
            k_c, v_c = z_ctx[..., :KV_DIM], z_ctx[..., KV_DIM:]
        else:
            z_ctx = h_ctx @ w_in[l]
            q_c = z_ctx[..., :ATTN_DIM]
            k_c = z_ctx[..., kv_lo:kv_lo + KV_DIM]
            v_c = z_ctx[..., kv_lo + KV_DIM:kv_hi]
            u_c = z_ctx[..., kv_hi:kv_hi + GMLP_DIM]
            g_c = z_ctx[..., kv_hi + GMLP_DIM:]

        qh_l = apply_rope(rmsnorm(q_heads(q_l), q_norm_g[l]), cos, sin) * q_scale
        kh_l = apply_rope(rmsnorm(kv_heads(k_l), k_norm_g[l]), cos, sin)
        kh_c = rmsnorm(kv_heads(k_c), k_norm_g[l])
        vh_c = kv_heads(v_c)
        k_all = jnp.concatenate([kh_c, kh_l], axis=2)
        v_all = jnp.concatenate([vh_c, kv_heads(v_l)], axis=2)
        attn_lat = gqa_blocks(qh_l, k_all, v_all)
        gm_lat = chunk_gmlp(u_l, g_l, w_spatial[l], b_spatial[l], gmlp_norm_g[l])
        x_lat = x_lat + ga1 * (jnp.concatenate([attn_lat, gm_lat], axis=-1) @ w_out[l])
        x_lat = x_lat + ga2 * sqrelu_mlp(modulate(rmsnorm(x_lat, norm2_g[l]), sh2, sc2),
                                         w_ff1[l], w_ff2[l])

        if not last:
            qh_c = rmsnorm(q_heads(q_c), q_norm_g[l]) * q_scale
            attn_ctx = gqa_blocks(qh_c, kh_c, vh_c)
            gm_ctx = chunk_gmlp(u_c, g_c, w_spatial[l], b_spatial[l], gmlp_norm_g[l])
            x_ctx = x_ctx + cga1 * (jnp.concatenate([attn_ctx, gm_ctx], axis=-1) @ w_out[l])
            x_ctx = x_ctx + cga2 * sqrelu_mlp(modulate(rmsnorm(x_ctx, norm2_g[l]), csh2, csc2),
                                              w_ff1[l], w_ff2[l])
    return x_lat
```

```python
import contextlib
import numpy as np
import concourse.bass as bass
import concourse.mybir as mybir
from concourse.bass_utils import run_bass_kernel_spmd

F32 = mybir.dt.float32
BF16 = mybir.dt.bfloat16
AF = mybir.ActivationFunctionType
ALU = mybir.AluOpType
AX = mybir.AxisListType

D = 1024
T = 2176
NTILE = 17
NKEY = 2 * T
NKC = 34
IN_DIM = 1792
EPS = 1e-6
BLOCKS = [(0, 128)] + [(128 + 512 * i, 512) for i in range(4)]
SEM_LIMIT = 30000


class Buf:
    __slots__ = ("w", "r")

    def __init__(self):
        self.w = None
        self.r = {}


class KB:
    def __init__(self, nc, es):
        self.nc = nc
        self.es = es
        self.E = {"pe": nc.tensor, "act": nc.scalar, "dve": nc.vector, "pool": nc.gpsimd, "sp": nc.sync}
        self.sem = {}
        self.cnt = {}
        self.nsem = 0
        for e in self.E:
            self._newsem(e)
        self.seen = {e: {} for e in self.E}
        self.slots = {}

    def _mk(self, name):
        self.nsem += 1
        return self.es.enter_context(self.nc.semaphore(f"{name}_{self.nsem}"))

    def _newsem(self, e):
        self.sem[e] = self._mk("s" + e)
        self.cnt[e] = 0

    def _wait(self, eng, deps):
        best = {}
        for tok in deps:
            if tok is None:
                continue
            s, v, src = tok
            if src == "pe" and eng == "pe":
                continue
            key = id(s)
            if self.seen[eng].get(key, 0) >= v:
                continue
            if key not in best or best[key][1] < v:
                best[key] = (s, v)
        for key, (s, v) in best.items():
            self.E[eng].wait_ge(s, v)
            self.seen[eng][key] = v

    def _signal(self, eng, ins):
        if self.cnt[eng] >= SEM_LIMIT:
            self._newsem(eng)
        self.cnt[eng] += 1
        ins.then_inc(self.sem[eng], 1)
        return (self.sem[eng], self.cnt[eng], eng)

    def _deps(self, reads, writes, extra):
        deps = list(extra)
        for b in reads:
            deps.append(b.w)
        for b in writes:
            deps.append(b.w)
            deps.extend(b.r.values())
        return deps

    def op(self, eng, fn, reads=(), writes=(), extra=()):
        self._wait(eng, self._deps(reads, writes, extra))
        tok = self._signal(eng, fn(self.E[eng]))
        for b in reads:
            b.r[eng] = tok
        for b in writes:
            b.w = tok
            b.r = {}
        return tok

    def group(self, eng, fns, reads=(), writes=(), extra=()):
        self._wait(eng, self._deps(reads, writes, extra))
        ins = None
        for fn in fns:
            ins = fn(self.E[eng])
        tok = self._signal(eng, ins)
        for b in reads:
            b.r[eng] = tok
        for b in writes:
            b.w = tok
            b.r = {}
        return tok

    def dma(self, q, slot, out, in_, reads=(), writes=(), extra=()):
        self._wait(q, self._deps(reads, writes, extra))
        if slot not in self.slots or self.slots[slot][1] >= SEM_LIMIT:
            self.slots[slot] = [self._mk("d" + slot), 0]
        st = self.slots[slot]
        st[1] += 16
        self.E[q].dma_start(out=out, in_=in_).then_inc(st[0], 16)
        tok = (st[0], st[1], "dma")
        for b in reads:
            b.r["dma_" + slot] = tok
        for b in writes:
            b.w = tok
            b.r = {}
        return tok

    def now(self, engs=("pe", "act", "dve")):
        return [(self.sem[e], self.cnt[e], e) for e in engs if self.cnt[e] > 0]

    def barrier(self, engs=("pe", "act", "dve")):
        toks = self.now(engs)
        for e in engs:
            self._wait(e, [t for t in toks if t[2] != e])

    def fence(self, eng, engs=("pe", "act", "dve")):
        self._wait(eng, self.now(engs))


class Rec:
    def __init__(self):
        self.items = []

    def op(self, *a, **kw):
        self.items.append(("op", a, kw))

    def group(self, *a, **kw):
        self.items.append(("group", a, kw))

    def dma(self, *a, **kw):
        self.items.append(("dma", a, kw))


class Switch:
    def __init__(self, real):
        self.real = real
        self.t = real

    def op(self, *a, **kw):
        return self.t.op(*a, **kw)

    def group(self, *a, **kw):
        return self.t.group(*a, **kw)

    def dma(self, *a, **kw):
        return self.t.dma(*a, **kw)

    def __getattr__(self, n):
        return getattr(self.real, n)


def _build(NL, STOP=9):
    nc = bass.Bass("TRN2", target_bir_lowering=False)
    dt_in = lambda n, s: nc.dram_tensor(n, s, F32, kind="ExternalInput").ap()
    xin = dt_in("xin", [T, D])
    cvec = dt_in("cvec", [2, D])
    rope = dt_in("rope", [T, 96])
    ident_d = dt_in("ident", [128, 128])
    w_mod = dt_in("w_mod", [4, D, 6 * D])
    b_mod = dt_in("b_mod", [4, 6 * D])
    norm1_g = dt_in("norm1_g", [4, D])
    w_in = dt_in("w_in", [4, D, IN_DIM])
    q_norm_g = dt_in("q_norm_g", [4, 64])
    k_norm_g = dt_in("k_norm_g", [4, 64])
    gmlp_norm_g = dt_in("gmlp_norm_g", [4, 512])
    w_spatial = dt_in("w_spatial", [4, 4, 128, 128])
    b_spatial = dt_in("b_spatial", [4, 4, 128])
    w_out = dt_in("w_out", [4, D, D])
    norm2_g = dt_in("norm2_g", [4, D])
    w_ff1 = dt_in("w_ff1", [4, D, 4 * D])
    w_ff2 = dt_in("w_ff2", [4, 4 * D, D])
    y = nc.dram_tensor("y", [2048, D], F32, kind="ExternalOutput").ap()
    ib = nc.dram_tensor("ib", [256, T], BF16)
    ob = nc.dram_tensor("ob", [512, T], BF16)

    es = contextlib.ExitStack()
    with es:
        sb = lambda n, s, d=F32: es.enter_context(nc.sbuf_tensor(n, s, d))
        XT = sb("XT", [128, 8, T])
        GM = sb("GM", [128, 4, T], BF16)
        RAO = sb("RAO", [128, 4 * T], BF16)
        R1 = sb("R1", [128, 6 * T + NKC * 130], BF16)
        WA = sb("WA", [128, 8 * IN_DIM], BF16)
        WB = sb("WB", [128, 8192], BF16)
        IDENT = sb("IDENT", [128, 128])
        IDENTB = sb("IDENTB", [128, 128], BF16)
        ONES = sb("ONES", [128, 128])
        MODT = sb("MODT", [128, 48, 2])
        MODN = sb("MODN", [128, 48, 2])
        PPV = sb("PPV", [128, 72])
        ROWS = sb("ROWS", [128, 128])
        GB = sb("GB", [128, 640])
        WST = sb("WST", [128, 4, 128], BF16)
        SC = sb("SC", [128, 8, 2], BF16)
        GE1 = sb("GE1", [128, 8, 2])
        GE2 = sb("GE2", [128, 8, 2])
        W512 = [sb(f"W512_{i}", [128, 512]) for i in range(7)]
        sq0, sq1, rstd, tmp0, tmp1, SQ, T1 = W512
        QR = sb("QR", [128, 512], BF16)
        VN = sb("VN", [128, 512], BF16)
        SQk = sb("SQk", [128, 128])
        T1k = sb("T1k", [128, 128])
        PPk = sb("PPk", [128, 128])
        KR = sb("KR", [128, 128], BF16)
        SM = sb("SM", [128, 64])
        RT = [sb(f"RT{i}", [128, 96]) for i in range(2)]
        PS = [es.enter_context(nc.psum_tensor(f"ps{i}", [128, 512], F32)) for i in range(3)]
        PSS = es.enter_context(nc.psum_tensor("pss", [128, 2048], F32))
        PS = PS + [PSS[:, 512 * i:512 * (i + 1)] for i in range(4)]
        PSB = es.enter_context(nc.psum_tensor("psb", [128, 1024], BF16))

        k = Switch(KB(nc, es))
        bSMq, bSMk, bSMg = Buf(), Buf(), Buf()
        win_loaded = set()
        bPS = [Buf() for _ in range(7)]
        bPSB = Buf()
        bW512 = [Buf() for _ in range(7)]
        b_sq0, b_sq1, b_rstd, b_tmp0, b_tmp1, b_SQ, b_T1 = bW512
        bQR, bVN, bSQk, bT1k, bPPk, bKR, bSM = (Buf() for _ in range(7))
        bRT = [Buf(), Buf()]
        bXT = [[Buf() for _ in BLOCKS] for _ in range(8)]
        bWA, bWB = Buf(), Buf()
        bC = Buf()
        bQT = [Buf() for _ in range(NTILE)]
        bGM = [Buf() for _ in range(NTILE)]
        bKVL = bWB
        bKT, bVA = Buf(), Buf()
        bAO = [[[Buf() for _ in BLOCKS] for _ in range(2)] for _ in range(4)]
        bH2 = [[Buf() for _ in BLOCKS] for _ in range(8)]
        bHT = [Buf() for _ in range(8)]
        bGU = Buf()
        bST = Buf()
        bIB, bOB = Buf(), Buf()

        QT = R1[:, 0:4 * T].rearrange("p (j t) -> p j t", j=4)
        KT = R1[:, 4 * T:6 * T]
        VA = R1[:, 6 * T:6 * T + NKC * 130].rearrange("p (c d) -> p c d", d=130)
        H2T = R1[:, 0:8 * T].rearrange("p (k t) -> p k t", k=8)
        KVL = WB[:, 0:2 * T].rearrange("p (a t) -> p a t", a=2)
        WSR = QR[:, :].rearrange("p (g q) -> p g q", g=4)
        GROW0, GROW1 = sq0, sq1
        AO = RAO[:, :].rearrange("p (j t) -> p j t", j=4)
        HTb = RAO[:, 0:4096].rearrange("p (k t) -> p k t", k=8)
        GU = RAO[:, 4096:8192].rearrange("p (g t) -> p g t", g=4)
        WM = [RAO[:, 0:4096].rearrange("p (k n) -> p k n", k=8),
              RAO[:, 4096:8192].rearrange("p (k n) -> p k n", k=8)]
        bWM = [Buf(), Buf()]
        bMODN = Buf()
        WMB = [WA[:, 4096 + 4096 * i:8192 + 4096 * i].rearrange("p (k n) -> p k n", k=8) for i in range(2)]
        bWMB = [Buf(), Buf()]
        AF2 = [RAO[:, 0:2048].rearrange("p (j t) -> p j t", j=4),
               RAO[:, 2048:4096].rearrange("p (j t) -> p j t", j=4)]
        bAF = [Buf(), Buf()]
        WIN = WA[:, :].rearrange("p (k n) -> p k n", k=8)
        PT = [WA[:, i * 512:(i + 1) * 512] for i in range(3)]
        bPT = [Buf() for _ in range(3)]
        XS = [sq0, sq1]
        WOA = WB[:, 0:4096].rearrange("p (j n) -> p j n", j=4)
        WOG = WB[:, 4096:8192].rearrange("p (g n) -> p g n", g=4)
        FW1 = [WA[:, 0:4096].rearrange("p (k n) -> p k n", k=8), WB[:, 0:4096].rearrange("p (k n) -> p k n", k=8)]
        FW2 = [WA[:, 4096:8192].rearrange("p (j n) -> p j n", j=4), WB[:, 4096:8192].rearrange("p (j n) -> p j n", j=4)]
        bFW = [bWA, bWB]

        k.dma("sp", "c0", IDENT[:, :], ident_d[:, :], writes=[bC])
        k.op("dve", lambda e: e.memset(ONES[:, :], 1.0), writes=[bC])
        k.op("act", lambda e: e.activation(out=IDENTB[:, :], in_=IDENT[:, :], func=AF.Copy), reads=[bC], writes=[bST])
        CV = ROWS
        k.dma("sp", "c1", sq0[0:2, :], cvec[:, 0:512], writes=[b_sq0])
        k.dma("sp", "c1b", sq1[0:2, :], cvec[:, 512:1024], writes=[b_sq1])
        k.op("act", lambda e: e.activation(out=tmp0[0:2, :], in_=sq0[0:2, :], func=AF.Silu), reads=[b_sq0], writes=[b_tmp0])
        k.op("act", lambda e: e.activation(out=tmp1[0:2, :], in_=sq1[0:2, :], func=AF.Silu), reads=[b_sq1], writes=[b_tmp1])
        fns = []
        for kk in range(8):
            src = (tmp0 if kk < 4 else tmp1)
            c0 = (kk % 4) * 128
            fns.append(lambda e, kk=kk, src=src, c0=c0: e.matmul(PS[0][:, 2 * kk:2 * kk + 2], src[0:2, c0:c0 + 128], IDENT[0:2, 0:2], start=True, stop=True))
        k.group("pe", fns, reads=[b_tmp0, b_tmp1, bC], writes=[bPS[0]])
        k.op("dve", lambda e: e.tensor_copy(out=SC[:, :, :], in_=PS[0][:, 0:16].rearrange("p (k s) -> p k s", s=2)), reads=[bPS[0]], writes=[bST])

        XSt = [SQ, T1]
        bXS = [b_SQ, b_T1]
        for tt in range(NTILE):
            s = tt % 2
            k.dma("sp", f"xs{s}", XSt[s][:, :], xin[tt * 128:(tt + 1) * 128, 0:512], writes=[bXS[s]])
            bi = 0 if tt == 0 else 1 + (tt - 1) // 4
            for hf in range(2):
                if hf == 1:
                    k.dma("sp", f"xs{s}", XSt[s][:, :], xin[tt * 128:(tt + 1) * 128, 512:1024], writes=[bXS[s]])
                pb = (2 * tt + hf) % 4
                fns = [lambda e, j=j, s=s, pb=pb: e.transpose(PS[pb][:, j * 128:(j + 1) * 128], XSt[s][:, j * 128:(j + 1) * 128], IDENT[:, :]) for j in range(4)]
                k.group("pe", fns, reads=[bXS[s], bC], writes=[bPS[pb]])
                eng = "act" if hf == 0 else "dve"
                dst = XT[:, hf * 4:(hf + 1) * 4, tt * 128:(tt + 1) * 128]
                srcv = PS[pb][:, :].rearrange("p (j t) -> p j t", j=4)
                if eng == "act":
                    k.op("act", lambda e, dst=dst, srcv=srcv: e.activation(out=dst, in_=srcv, func=AF.Copy),
                         reads=[bPS[pb]], writes=[bXT[kk][bi] for kk in range(hf * 4, hf * 4 + 4)])
                else:
                    k.op("dve", lambda e, dst=dst, srcv=srcv: e.tensor_copy(out=dst, in_=srcv),
                         reads=[bPS[pb]], writes=[bXT[kk][bi] for kk in range(hf * 4, hf * 4 + 4)])

        def norm_block(bi, GE, sh_base, dst_fn, dst_bufs_fn):
            bs, bn = BLOCKS[bi]
            sidx = 1 if bi == 0 else 0
            sqs, bsq = [sq0, sq1], [b_sq0, b_sq1]
            for kk in range(8):
                s = kk % 2
                k.op("act", lambda e, kk=kk, s=s: e.activation(out=sqs[s][:, :bn], in_=XT[:, kk, bs:bs + bn], func=AF.Square),
                     reads=[bXT[kk][bi]], writes=[bsq[s]])
                k.op("pe", lambda e, kk=kk, s=s: e.matmul(PS[0][:, :bn], ONES[:, :], sqs[s][:, :bn], start=(kk == 0), stop=(kk == 7)),
                     reads=[bsq[s]], writes=[bPS[0]])
            k.op("act", lambda e: e.activation(out=rstd[:, :bn], in_=PS[0][:, :bn], func=AF.Sqrt, scale=1.0 / D, bias=EPS),
                 reads=[bPS[0]], writes=[b_rstd])
            k.op("dve", lambda e: e.reciprocal(out=rstd[:, :bn], in_=rstd[:, :bn]), reads=[b_rstd], writes=[b_rstd])
            tms, btm = [tmp0, tmp1], [b_tmp0, b_tmp1]
            for kk in range(8):
                s = kk % 2
                k.op("dve", lambda e, kk=kk, s=s: e.scalar_tensor_tensor(out=tms[s][:, :bn], in0=XT[:, kk, bs:bs + bn],
                                                                         scalar=GE[:, kk, sidx:sidx + 1], in1=rstd[:, :bn],
                                                                         op0=ALU.mult, op1=ALU.mult),
                     reads=[bXT[kk][bi], b_rstd, bC], writes=[btm[s]])
                k.op("act", lambda e, kk=kk, s=s: e.activation(out=dst_fn(kk), in_=tms[s][:, :bn], func=AF.Identity,
                                                               bias=MODT[:, sh_base + kk, sidx:sidx + 1], scale=1.0),
                     reads=[btm[s], bC], writes=dst_bufs_fn(kk))

        def rms_rope(PSsrc, bps, ncol, nh, SQt, bSQt, T1t, bT1t, PPt, bPPt, GBv, rt, brt, out_ap_fn, bout, ss_off, bSM):
            SS = SM[:, ss_off:ss_off + nh]
            k.op("act", lambda e: e.activation(out=SQt[:, :ncol], in_=PSsrc, func=AF.Square), reads=[bps], writes=[bSQt])
            k.op("dve", lambda e: e.tensor_reduce(out=SS, in_=SQt[:, :ncol].rearrange("p (h d) -> p h d", h=nh), axis=AX.X, op=ALU.add),
                 reads=[bSQt], writes=[bSM])
            k.op("act", lambda e: e.activation(out=SS, in_=SS, func=AF.Sqrt, scale=1.0 / 64, bias=EPS), reads=[bSM], writes=[bSM])
            k.op("dve", lambda e: e.reciprocal(out=SS, in_=SS), reads=[bSM], writes=[bSM])
            k.op("dve", lambda e: e.tensor_tensor(out=T1t[:, :ncol].rearrange("p (h d) -> p h d", h=nh),
                                                  in0=PSsrc.rearrange("p (h d) -> p h d", h=nh),
                                                  in1=SS.unsqueeze(2).broadcast_to([128, nh, 64]), op=ALU.mult),
                 reads=[bps, bSM], writes=[bT1t])
            k.op("dve", lambda e: e.tensor_tensor(out=T1t[:, :ncol].rearrange("p (h d) -> p h d", h=nh),
                                                  in0=T1t[:, :ncol].rearrange("p (h d) -> p h d", h=nh),
                                                  in1=GBv.unsqueeze(1).broadcast_to([128, nh, 64]), op=ALU.mult),
                 reads=[bT1t, bC], writes=[bT1t])
            v5 = lambda t: t[:, :ncol].rearrange("p (h a f d) -> p h a f d", h=nh, a=2, f=2)
            cosb = rt[:, 0:32].rearrange("p (a d) -> p a d", a=2).unsqueeze(1).broadcast_to([128, nh, 2, 16])
            sinb = rt[:, 32:64].rearrange("p (a d) -> p a d", a=2).unsqueeze(1).broadcast_to([128, nh, 2, 16])
            nsinb = rt[:, 64:96].rearrange("p (a d) -> p a d", a=2).unsqueeze(1).broadcast_to([128, nh, 2, 16])
            k.op("dve", lambda e: e.tensor_tensor(out=v5(PPt)[:, :, :, 0, :], in0=v5(T1t)[:, :, :, 1, :], in1=nsinb, op=ALU.mult),
                 reads=[bT1t, brt], writes=[bPPt])
            k.op("dve", lambda e: e.tensor_tensor(out=v5(PPt)[:, :, :, 1, :], in0=v5(T1t)[:, :, :, 0, :], in1=sinb, op=ALU.mult),
                 reads=[bT1t, brt], writes=[bPPt])
            for f in range(2):
                k.op("dve", lambda e, f=f: e.tensor_tensor(out=v5(T1t)[:, :, :, f, :], in0=v5(T1t)[:, :, :, f, :], in1=cosb, op=ALU.mult),
                     reads=[bPPt, brt], writes=[bT1t])
            o, a, b = out_ap_fn(T1t, PPt)
            k.op("dve", lambda e: e.tensor_tensor(out=o, in0=a, in1=b, op=ALU.add), reads=[bT1t, bPPt], writes=[bout])

        for l in range(NL):
            k.barrier()
            if STOP < 1:
                continue
            k.fence("sp")
            k.fence("pool")
            k.dma("sp", "c2", ROWS[0:8, :], norm1_g[l].rearrange("(k p) -> k p", p=128), writes=[bC])
            k.dma("sp", "c2", ROWS[8:16, :], norm2_g[l].rearrange("(k p) -> k p", p=128), writes=[bC])
            k.dma("sp", "c2", ROWS[16:20, :], gmlp_norm_g[l].rearrange("(k p) -> k p", p=128), writes=[bC])
            k.dma("sp", "c2", ROWS[20:68, :], b_mod[l].rearrange("(k p) -> k p", p=128), writes=[bC])
            k.dma("sp", "c2b", GROW0[0:1, 0:64], q_norm_g[l:l + 1, :], writes=[b_sq0])
            k.dma("sp", "c2b", GROW0[0:1, 64:128], k_norm_g[l:l + 1, :], writes=[b_sq0])
            k.dma("sp", "c2b", GROW0[0:1, 128:512], b_spatial[l:l + 1, 0:3, :].rearrange("o g p -> o (g p)"), writes=[b_sq0])
            k.dma("sp", "c2c", GROW1[0:1, 0:128], b_spatial[l:l + 1, 3, :], writes=[b_sq1])
            k.dma("pool", "c3", WSR[:, :, :], w_spatial[l].rearrange("g p q -> p g q"), writes=[bQR])
            k.op("pe", lambda e: e.matmul(PS[1][:, 0:68], ROWS[0:68, :], IDENT[0:68, 0:68], start=True, stop=True), reads=[bC], writes=[bPS[1]])
            k.op("dve", lambda e: e.tensor_copy(out=PPV[:, 0:68], in_=PS[1][:, 0:68]), reads=[bPS[1]], writes=[bC])
            k.group("pe", [lambda e: e.matmul(PS[2][:, 0:512], ONES[0:1, :], GROW0[0:1, 0:512], start=True, stop=True),
                           lambda e: e.matmul(PS[3][:, 0:128], ONES[0:1, :], GROW1[0:1, 0:128], start=True, stop=True)],
                    reads=[bC, b_sq0, b_sq1], writes=[bPS[2], bPS[3]])
            k.op("dve", lambda e: e.tensor_copy(out=GB[:, 0:512], in_=PS[2][:, 0:512]), reads=[bPS[2]], writes=[bC])
            k.op("dve", lambda e: e.tensor_copy(out=GB[:, 512:640], in_=PS[3][:, 0:128]), reads=[bPS[3]], writes=[bC])
            k.op("dve", lambda e: e.tensor_scalar(out=GB[:, 0:64], in0=GB[:, 0:64], scalar1=0.125, scalar2=None, op0=ALU.mult), reads=[bC], writes=[bC])
            k.group("pe", [lambda e, g=g: e.transpose(PSB[:, g * 128:(g + 1) * 128], WSR[:, g, :], IDENTB[:, :]) for g in range(4)],
                    reads=[bQR, bST], writes=[bPSB])
            k.op("act", lambda e: e.activation(out=WST[:, :, :], in_=PSB[:, 0:512].rearrange("p (g q) -> p g q", g=4), func=AF.Copy),
                 reads=[bPSB], writes=[bC])
            def mod_dma(ln, g, bufs, bbufs, tag):
                s_ = g % 2
                for kk in range(8):
                    k.dma("pool", f"{tag}{s_}", bufs[s_][:, kk, :], w_mod[ln, kk * 128:(kk + 1) * 128, g * 512:(g + 1) * 512], writes=[bbufs[s_]])

            def mod_mm(g, bufs, bbufs, pb, extra_reads):
                s_ = g % 2
                fns = []
                for c4 in range(4):
                    for kk in range(8):
                        fns.append(lambda e, c4=c4, kk=kk: e.matmul(PS[pb][:, 2 * c4:2 * c4 + 2], bufs[s_][:, kk, c4 * 128:(c4 + 1) * 128],
                                                                    SC[:, kk, :], start=(kk == 0), stop=(kk == 7)))
                k.group("pe", fns, reads=[bbufs[s_], bST] + extra_reads, writes=[bPS[pb]])
                k.op("dve", lambda e: e.tensor_copy(out=MODN[:, g * 4:(g + 1) * 4, :], in_=PS[pb][:, 0:8].rearrange("p (c s) -> p c s", s=2)),
                     reads=[bPS[pb]], writes=[bMODN])

            if l == 0:
                for g in range(12):
                    mod_dma(0, g, WM, bWM, "wm")
                    mod_mm(g, WM, bWM, 4, [])
            k.op("dve", lambda e: e.tensor_tensor(out=MODT[:, :, :], in0=MODN[:, :, :],
                                                  in1=PPV[:, 20:68].unsqueeze(2).broadcast_to([128, 48, 2]), op=ALU.add),
                 reads=[bMODN, bC], writes=[bC])
            k.op("dve", lambda e: e.scalar_tensor_tensor(out=GE1[:, :, :], in0=MODT[:, 8:16, :], scalar=1.0,
                                                         in1=PPV[:, 0:8].unsqueeze(2).broadcast_to([128, 8, 2]), op0=ALU.add, op1=ALU.mult),
                 reads=[bC], writes=[bC])
            k.op("dve", lambda e: e.scalar_tensor_tensor(out=GE2[:, :, :], in0=MODT[:, 32:40, :], scalar=1.0,
                                                         in1=PPV[:, 8:16].unsqueeze(2).broadcast_to([128, 8, 2]), op0=ALU.add, op1=ALU.mult),
                 reads=[bC], writes=[bC])
            k.barrier()

            if STOP < 2:
                continue
            if l not in win_loaded:
                k.fence("pool")
                for kk in range(8):
                    k.dma("pool", "win", WIN[:, kk, :], w_in[l, kk * 128:(kk + 1) * 128, :], writes=[bWA])

            for bi, (bs, bn) in enumerate(BLOCKS):
                sidx = 1 if bi == 0 else 0
                norm_block(bi, GE1, 0, lambda kk: HTb[:, kk, :bn], lambda kk: [bHT[kk]])
                for uc in range(4):
                    pb = 5 + (uc % 2)
                    fns = [lambda e, kk=kk, uc=uc, pb=pb: e.matmul(PS[pb][:, :bn], WIN[:, kk, 768 + uc * 128:768 + (uc + 1) * 128], HTb[:, kk, :bn],
                                                                 start=(kk == 0), stop=(kk == 7)) for kk in range(8)]
                    k.group("pe", fns, reads=bHT + [bWA], writes=[bPS[pb]])
                    k.op("act", lambda e, uc=uc, pb=pb: e.activation(out=GU[:, uc, :bn], in_=PS[pb][:, :bn], func=AF.Gelu_apprx_tanh),
                         reads=[bPS[pb]], writes=[bGU])
                for tt in range(bn // 128):
                    gt = bs // 128 + tt
                    tsl = slice(tt * 128, (tt + 1) * 128)
                    gsl = slice(gt * 128, (gt + 1) * 128)
                    r = gt % 2
                    k.dma("sp", f"rt{r}", RT[r][:, :], rope[gt * 128:(gt + 1) * 128, :], writes=[bRT[r]])
                    for (pb, c0, ncol) in ((1, 0, 512), (2, 512, 256), (3, 1280, 512)):
                        fns = [lambda e, kk=kk, pb=pb, c0=c0, ncol=ncol: e.matmul(PS[pb][:, :ncol], HTb[:, kk, tsl], WIN[:, kk, c0:c0 + ncol],
                                                                                start=(kk == 0), stop=(kk == 7)) for kk in range(8)]
                        k.group("pe", fns, reads=bHT + [bWA], writes=[bPS[pb]])
                    rq = Rec()
                    k.t = rq
                    rms_rope(PS[1][:, 0:512], bPS[1], 512, 8, SQ, b_SQ, T1, b_T1, tmp0, b_tmp0, GB[:, 0:64], RT[r], bRT[r],
                             lambda A, B: (QR[:, :].rearrange("p (j s d) -> p s j d", j=4, s=2),
                                           A[:, :].rearrange("p (s j d) -> p s j d", s=2, j=4),
                                           B[:, :].rearrange("p (s j d) -> p s j d", s=2, j=4)), bQR, 0, bSMq)
                    k.group("pe", [lambda e, j=j: e.transpose(PSB[:, j * 128:(j + 1) * 128], QR[:, j * 128:(j + 1) * 128], IDENTB[:, :]) for j in range(4)],
                            reads=[bQR, bST], writes=[bPSB])
                    k.op("act", lambda e, gsl=gsl: e.activation(out=QT[:, :, gsl], in_=PSB[:, 0:512].rearrange("p (j t) -> p j t", j=4), func=AF.Copy),
                         reads=[bPSB], writes=[bQT[gt]])
                    rk = Rec()
                    k.t = rk
                    rms_rope(PS[2][:, 0:128], bPS[2], 128, 2, SQk, bSQk, T1k, bT1k, PPk, bPPk, GB[:, 64:128], RT[r], bRT[r],
                             lambda A, B: (KR[:, :], A[:, :], B[:, :]), bKR, 8, bSMk)
                    k.op("pe", lambda e: e.transpose(PSB[:, 512:640], KR[:, :], IDENTB[:, :]), reads=[bKR, bST], writes=[bPSB])
                    k.op("act", lambda e, gsl=gsl: e.activation(out=KVL[:, 0, gsl], in_=PSB[:, 512:640], func=AF.Copy), reads=[bPSB], writes=[bKVL])
                    k.op("act", lambda e, gsl=gsl: e.activation(out=KVL[:, 1, gsl], in_=PS[2][:, 128:256], func=AF.Copy), reads=[bPS[2]], writes=[bKVL])
                    rg = Rec()
                    k.t = rg
                    bSM = bSMg
                    GG, bGG, TT, bTT = sq0, b_sq0, sq1, b_sq1
                    k.op("act", lambda e: e.activation(out=GG[:, :], in_=PS[3][:, :], func=AF.Gelu_apprx_tanh), reads=[bPS[3]], writes=[bGG])
                    S1, S2, MEAN, MSQ = SM[:, 16:20], SM[:, 20:24], SM[:, 24:28], SM[:, 28:32]
                    k.op("dve", lambda e: e.tensor_reduce(out=S1, in_=GG[:, :].rearrange("p (g c) -> p g c", g=4), axis=AX.X, op=ALU.add), reads=[bGG], writes=[bSM])
                    k.op("act", lambda e: e.activation(out=TT[:, :], in_=GG[:, :], func=AF.Square), reads=[bGG], writes=[bTT])
                    k.op("dve", lambda e: e.tensor_reduce(out=S2, in_=TT[:, :].rearrange("p (g c) -> p g c", g=4), axis=AX.X, op=ALU.add), reads=[bTT], writes=[bSM])
                    k.op("dve", lambda e: e.tensor_scalar(out=MEAN, in0=S1, scalar1=1.0 / 128, scalar2=None, op0=ALU.mult), reads=[bSM], writes=[bSM])
                    k.op("dve", lambda e: e.tensor_tensor(out=MSQ, in0=MEAN, in1=MEAN, op=ALU.mult), reads=[bSM], writes=[bSM])
                    k.op("dve", lambda e: e.scalar_tensor_tensor(out=S2, in0=S2, scalar=1.0 / 128, in1=MSQ, op0=ALU.mult, op1=ALU.subtract), reads=[bSM], writes=[bSM])
                    k.op("act", lambda e: e.activation(out=S2, in_=S2, func=AF.Sqrt, scale=1.0, bias=EPS), reads=[bSM], writes=[bSM])
                    k.op("dve", lambda e: e.reciprocal(out=S2, in_=S2), reads=[bSM], writes=[bSM])
                    g3 = lambda t: t[:, :].rearrange("p (g c) -> p g c", g=4)
                    k.op("dve", lambda e: e.tensor_tensor(out=g3(GG), in0=g3(GG), in1=MEAN.unsqueeze(2).broadcast_to([128, 4, 128]), op=ALU.subtract),
                         reads=[bSM], writes=[bGG])
                    k.op("dve", lambda e: e.tensor_tensor(out=g3(VN), in0=g3(GG), in1=S2.unsqueeze(2).broadcast_to([128, 4, 128]), op=ALU.mult),
                         reads=[bGG, bSM], writes=[bVN])
                    k.group("pe", [lambda e, g=g: e.matmul(PS[4][:, g * 128:(g + 1) * 128], VN[:, g * 128:(g + 1) * 128], WST[:, g, :], start=True, stop=True)
                                   for g in range(4)], reads=[bVN, bC], writes=[bPS[4]])
                    for g in range(4):
                        k.op("dve", lambda e, g=g: e.scalar_tensor_tensor(out=TT[:, g * 128:(g + 1) * 128], in0=PS[4][:, g * 128:(g + 1) * 128],
                                                                          scalar=PPV[:, 16 + g:17 + g], in1=GB[:, 128 + g * 128:128 + (g + 1) * 128],
                                                                          op0=ALU.mult, op1=ALU.add),
                             reads=[bPS[4], bC], writes=[bTT])
                    k.op("dve", lambda e, gsl=gsl, tsl=tsl: e.tensor_tensor(out=GM[:, :, gsl], in0=g3(TT), in1=GU[:, :, tsl], op=ALU.mult),
                         reads=[bTT, bGU], writes=[bGM[gt]])
                    k.t = k.real
                    chains = [rq.items, rk.items, rg.items]
                    pos = [0, 0, 0]
                    while any(pos[i] < len(chains[i]) for i in range(3)):
                        for i in range(3):
                            if pos[i] < len(chains[i]):
                                kind, a_, kw_ = chains[i][pos[i]]
                                pos[i] += 1
                                getattr(k.real, kind)(*a_, **kw_)

            if STOP < 3:
                continue
            k.fence("pool")
            k.dma("pool", "ib", ib[0:128, :], KVL[:, 0, :], reads=[bKVL], writes=[bIB])
            k.dma("pool", "ib", ib[128:256, :], KVL[:, 1, :], reads=[bKVL], writes=[bIB])
            k._wait("pool", [bIB.w, bOB.w] + list(bOB.r.values()))
            tokc = k._signal("pool", nc.gpsimd.collective_compute("AllGather", ALU.bypass, replica_groups=[[0, 1], [2, 3], [4, 5], [6, 7]],
                                                                  ins=[ib.ap().opt()], outs=[ob.ap().opt()]))
            bOB.w = tokc
            bOB.r = {}
            k.barrier()
            k.fence("pool")
            for rk in range(2):
                k.dma("pool", "kt", KT[:, rk * T:(rk + 1) * T], ob[rk * 256:rk * 256 + 128, :], reads=[bOB], writes=[bKT])
                vsrc = ob[rk * 256 + 128:rk * 256 + 256, :].rearrange("p (t d) -> p t d", d=128)
                k.dma("pool", "kv", VA[:, rk * 17:(rk + 1) * 17, 0:64], vsrc[:, :, 0:64], reads=[bOB], writes=[bVA])
                k.dma("pool", "kv", VA[:, rk * 17:(rk + 1) * 17, 65:129], vsrc[:, :, 64:128], reads=[bOB], writes=[bVA])
            k.op("dve", lambda e: e.memset(VA[:, :, 64:65], 1.0), writes=[bVA])
            k.op("dve", lambda e: e.memset(VA[:, :, 129:130], 1.0), writes=[bVA])
            for s2 in range(2):
                for j in range(4):
                    r0 = (s2 * 4 + j) * 64
                    k.dma("pool", "wo", WOA[s2 * 64:(s2 + 1) * 64, j, :], w_out[l, r0:r0 + 64, :], writes=[bWB])
            for g in range(4):
                k.dma("pool", "wo", WOG[:, g, :], w_out[l, 512 + g * 128:512 + (g + 1) * 128, :], writes=[bWB])

            if STOP < 4:
                continue
            RCs, bRCs = [rstd, sq0], [b_rstd, b_sq0]
            BCSs, bBCSs = [tmp0, tmp1], [b_tmp0, b_tmp1]
            PT2 = [WA[:, u * 1024:(u + 1) * 1024].rearrange("p (a n) -> p a n", a=2) for u in range(3)]
            SP2 = [PSS[:, u * 1024:(u + 1) * 1024].rearrange("p (a n) -> p a n", a=2) for u in range(2)]

            def finalize1(bi, bs, bn, j, kvh):
                po = kvh
                RC, bRC = RCs[po], bRCs[po]
                k.op("dve", lambda e: e.reciprocal(out=RC[64:65, :bn], in_=PS[po][64:65, :bn]), reads=[bPS[po]], writes=[bRC])

            def finalize(bi, bs, bn, j, kvh):
                po = kvh
                RC, bRC, BCS, bBCS = RCs[po], bRCs[po], BCSs[po], bBCSs[po]
                k.op("pe", lambda e: e.matmul(PS[2][0:64, :bn], ONES[64:65, 0:64], RC[64:65, :bn], start=True, stop=True), reads=[bRC, bC], writes=[bPS[2]])
                k.op("dve", lambda e: e.tensor_copy(out=BCS[0:64, :bn], in_=PS[2][0:64, :bn]), reads=[bPS[2]], writes=[bBCS])
                if kvh == 0:
                    k.op("dve", lambda e: e.tensor_tensor(out=AO[0:64, j, bs:bs + bn], in0=PS[po][0:64, :bn], in1=BCS[0:64, :bn], op=ALU.mult),
                         reads=[bPS[po], bBCS], writes=[bAO[j][0][bi]])
                else:
                    stg, bstg = ([QR, VN][j % 2], [bQR, bVN][j % 2])
                    k.op("dve", lambda e: e.tensor_tensor(out=stg[0:64, :bn], in0=PS[po][0:64, :bn], in1=BCS[0:64, :bn], op=ALU.mult),
                         reads=[bPS[po], bBCS], writes=[bstg])
                    k.dma("sp", f"ao{j % 2}", AO[64:128, j, bs:bs + bn], stg[0:64, :bn], reads=[bstg], writes=[bAO[j][1][bi]])

            pending = [None]
            for bi, (bs, bn) in enumerate(BLOCKS):
                kcs = [0, 17] if bi == 0 else list(range(NKC))
                qbufs = [bQT[bs // 128 + i] for i in range(bn // 128)]
                n = len(kcs)
                for j in range(4):
                    def S2(u):
                        sp, c = u % 2, kcs[u]
                        fns = [lambda e, a=a: e.matmul(PS[3 + 2 * sp + a][:, :bn], KT[a * 64:(a + 1) * 64, c * 128:(c + 1) * 128],
                                                       QT[a * 64:(a + 1) * 64, j, bs:bs + bn], start=True, stop=True) for a in range(2)]
                        k.group("pe", fns, reads=qbufs + [bKT], writes=[bPS[3 + 2 * sp], bPS[4 + 2 * sp]])

                    def EX2(u):
                        sp, pt = u % 2, u % 3
                        k.op("act", lambda e: e.activation(out=PT2[pt][:, :, :bn], in_=SP2[sp][:, :, :bn], func=AF.Exp),
                             reads=[bPS[3 + 2 * sp], bPS[4 + 2 * sp], bWA], writes=[bPT[pt]])

                    def PV2(u):
                        pt, c = u % 3, kcs[u]
                        fns = [lambda e, a=a: e.matmul(PS[a][0:65, :bn], VA[:, c, a * 65:(a + 1) * 65], PT2[pt][:, a, :bn],
                                                       start=(u == 0), stop=(u == n - 1)) for a in range(2)]
                        k.group("pe", fns, reads=[bPT[pt], bVA], writes=[bPS[0], bPS[1]])

                    S2(0)
                    S2(1)
                    if pending[0] is not None:
                        for a in range(2):
                            finalize1(*pending[0], a)
                        for a in range(2):
                            finalize(*pending[0], a)
                        pending[0] = None
                    for u in range(n):
                        EX2(u)
                        if u + 2 < n:
                            S2(u + 2)
                        PV2(u)
                    pending[0] = (bi, bs, bn, j)
                    if l + 1 < NL and bi >= 1:
                        slot_i = (bi - 1) * 4 + j
                        if slot_i == 0:
                            k.fence("pool")
                        if slot_i < 12:
                            mod_dma(l + 1, slot_i, WMB, bWMB, "wn")
                        if 1 <= slot_i <= 12:
                            mod_mm(slot_i - 1, WMB, bWMB, 2, [bWA])
            for a in range(2):
                finalize1(*pending[0], a)
            for a in range(2):
                finalize(*pending[0], a)
            k.barrier()
            k._wait("pe", [(k.slots[n_][0], k.slots[n_][1], "dma") for n_ in ("ao0", "ao1") if n_ in k.slots])

            if STOP < 5:
                continue
            for bi, (bs, bn) in enumerate(BLOCKS):
                sidx = 1 if bi == 0 else 0
                gmb = [bGM[bs // 128 + i] for i in range(bn // 128)]
                for dch in range(8):
                    pb = dch % 4
                    dsl = slice(dch * 128, (dch + 1) * 128)
                    fns = [lambda e, j=j, pb=pb, dsl=dsl: e.matmul(PS[pb][:, :bn], WOA[:, j, dsl], AO[:, j, bs:bs + bn], start=(j == 0), stop=False) for j in range(4)]
                    fns += [lambda e, g=g, pb=pb, dsl=dsl: e.matmul(PS[pb][:, :bn], WOG[:, g, dsl], GM[:, g, bs:bs + bn], start=False, stop=(g == 3)) for g in range(4)]
                    k.group("pe", fns, reads=[bWB] + gmb + [bAO[j][s2][bi] for j in range(4) for s2 in range(2)], writes=[bPS[pb]])
                    k.op("dve", lambda e, dch=dch, pb=pb: e.scalar_tensor_tensor(out=XT[:, dch, bs:bs + bn], in0=PS[pb][:, :bn], scalar=MODT[:, 16 + dch, sidx:sidx + 1],
                                                                                 in1=XT[:, dch, bs:bs + bn], op0=ALU.mult, op1=ALU.add),
                         reads=[bPS[pb], bC], writes=[bXT[dch][bi]])
            k.barrier()

            if STOP < 6:
                continue
            def load_ffn(g8):
                s = g8 % 2
                for kk in range(8):
                    k.dma("pool", f"fw{s}", FW1[s][:, kk, :], w_ff1[l, kk * 128:(kk + 1) * 128, g8 * 512:(g8 + 1) * 512], writes=[bFW[s]])
                for jj in range(4):
                    k.dma("pool", f"fw{s}", FW2[s][:, jj, :], w_ff2[l, g8 * 512 + jj * 128:g8 * 512 + (jj + 1) * 128, :], writes=[bFW[s]])

            k.fence("pool")
            load_ffn(0)
            for bi, (bs, bn) in enumerate(BLOCKS):
                norm_block(bi, GE2, 24, lambda kk, bs=bs, bn=bn: H2T[:, kk, bs:bs + bn], lambda kk, bi=bi: [bH2[kk][bi]])
            Rt, bRt = [SQ, T1], [b_SQ, b_T1]
            import os
            NG = int(os.environ.get('KNG', '8'))
            for g8 in range(NG):
                s = g8 % 2
                if g8 + 1 < NG:
                    load_ffn(g8 + 1)
                for bi, (bs, bn) in enumerate(BLOCKS):
                    sidx = 1 if bi == 0 else 0
                    ab = bi % 2
                    for jj in range(4):
                        pb = 1 + (jj % 2)
                        fns = [lambda e, kk=kk, jj=jj, pb=pb: e.matmul(PS[pb][:, :bn], FW1[s][:, kk, jj * 128:(jj + 1) * 128], H2T[:, kk, bs:bs + bn],
                                                                     start=(kk == 0), stop=(kk == 7)) for kk in range(8)]
                        k.group("pe", fns, reads=[bFW[s]] + [bH2[kk][bi] for kk in range(8)], writes=[bPS[pb]])
                        rr = jj % 2
                        k.op("act", lambda e, pb=pb, rr=rr: e.activation(out=Rt[rr][:, :bn], in_=PS[pb][:, :bn], func=AF.Relu), reads=[bPS[pb]], writes=[bRt[rr]])
                        k.op("dve", lambda e, jj=jj, rr=rr: e.tensor_tensor(out=AF2[ab][:, jj, :bn], in0=Rt[rr][:, :bn], in1=Rt[rr][:, :bn], op=ALU.mult),
                             reads=[bRt[rr]], writes=[bAF[ab]])
                    for dch in range(8):
                        pb = 3 + (dch % 3)
                        dsl = slice(dch * 128, (dch + 1) * 128)
                        fns = [lambda e, jj=jj, pb=pb, dsl=dsl: e.matmul(PS[pb][:, :bn], FW2[s][:, jj, dsl], AF2[ab][:, jj, :bn], start=(jj == 0), stop=(jj == 3)) for jj in range(4)]
                        k.group("pe", fns, reads=[bFW[s], bAF[ab]], writes=[bPS[pb]])
                        k.op("dve", lambda e, dch=dch, pb=pb: e.scalar_tensor_tensor(out=XT[:, dch, bs:bs + bn], in0=PS[pb][:, :bn], scalar=MODT[:, 40 + dch, sidx:sidx + 1],
                                                                                     in1=XT[:, dch, bs:bs + bn], op0=ALU.mult, op1=ALU.add),
                             reads=[bPS[pb], bC], writes=[bXT[dch][bi]])
                if g8 == 6 and NG == 8 and l + 1 < NL:
                    for kk in range(8):
                        k.dma("pool", "win", WIN[:, kk, :], w_in[l + 1, kk * 128:(kk + 1) * 128, :], writes=[bWA])
                    win_loaded.add(l + 1)
            k.barrier()

        k.fence("sp")
        OSt, bOS = [SQ, T1], [b_SQ, b_T1]
        ydone = []
        for tt in range(1, NTILE):
            bi = 1 + (tt - 1) // 4
            for hf in range(2):
                s = (2 * tt + hf) % 2
                pb = (2 * tt + hf) % 4
                fns = [lambda e, j=j, hf=hf, pb=pb: e.transpose(PS[pb][:, j * 128:(j + 1) * 128], XT[:, hf * 4 + j, tt * 128:(tt + 1) * 128], IDENT[:, :]) for j in range(4)]
                k.group("pe", fns, reads=[bXT[kk][bi] for kk in range(hf * 4, hf * 4 + 4)] + [bC], writes=[bPS[pb]])
                if hf == 0:
                    k.op("act", lambda e, s=s, pb=pb: e.activation(out=OSt[s][:, :], in_=PS[pb][:, :], func=AF.Copy), reads=[bPS[pb]], writes=[bOS[s]])
                else:
                    k.op("dve", lambda e, s=s, pb=pb: e.tensor_copy(out=OSt[s][:, :], in_=PS[pb][:, :]), reads=[bPS[pb]], writes=[bOS[s]])
                ydone.append(k.dma("sp", f"y{s}", y[(tt - 1) * 128:tt * 128, hf * 512:(hf + 1) * 512], OSt[s][:, :], reads=[bOS[s]]))
        k._wait("sp", ydone)
        k._wait("act", ydone)
    return nc


_NC_CACHE = {}


def _rope_table(half):
    tab = np.zeros((T, 96), np.float32)
    tab[:128, 0:32] = 1.0
    t = (half * 2048 + np.arange(2048)).astype(np.int64)
    row = (t // 64).astype(np.float32)
    col = (t % 64).astype(np.float32)
    inv = (np.float32(10000.0) ** (-np.arange(16, dtype=np.float32) / np.float32(16))).astype(np.float32)
    ang = np.concatenate([row[:, None] * inv[None, :], col[:, None] * inv[None, :]], axis=1).astype(np.float32)
    tab[128:, 0:32] = np.cos(ang)
    tab[128:, 32:64] = np.sin(ang)
    tab[128:, 64:96] = -np.sin(ang)
    return tab


def kernel(x, c, ctx, c_ctx, w_mod, b_mod, norm1_g, w_in, q_norm_g, k_norm_g, gmlp_norm_g,
           w_spatial, b_spatial, w_out, norm2_g, w_ff1, w_ff2, _nl=4, _stop=9):
    f = lambda a: np.ascontiguousarray(np.asarray(a, dtype=np.float32))
    x, c, ctx, c_ctx = f(x), f(c), f(ctx), f(c_ctx)
    shared = dict(w_mod=f(w_mod), b_mod=f(b_mod), norm1_g=f(norm1_g), w_in=f(w_in), q_norm_g=f(q_norm_g),
                  k_norm_g=f(k_norm_g), gmlp_norm_g=f(gmlp_norm_g), w_spatial=f(w_spatial), b_spatial=f(b_spatial),
                  w_out=f(w_out), norm2_g=f(norm2_g), w_ff1=f(w_ff1), w_ff2=f(w_ff2),
                  ident=np.eye(128, dtype=np.float32))
    if (_nl, _stop) not in _NC_CACHE:
        _NC_CACHE[(_nl, _stop)] = _build(_nl, _stop)
    nc = _NC_CACHE[(_nl, _stop)]
    in_maps = []
    for core in range(8):
        b, half = core // 2, core % 2
        xin = np.concatenate([ctx[b, half * 128:(half + 1) * 128], x[b, half * 2048:(half + 1) * 2048]], axis=0)
        m = dict(shared)
        m["xin"] = np.ascontiguousarray(xin)
        m["cvec"] = np.ascontiguousarray(np.stack([c[b], c_ctx], axis=0))
        m["rope"] = _rope_table(half)
        in_maps.append(m)
    res = run_bass_kernel_spmd(nc, in_maps, core_ids=list(range(8)))
    out = np.empty((4, 4096, D), np.float32)
    for core in range(8):
        b, half = core // 2, core % 2
        out[b, half * 2048:(half + 1) * 2048] = res.results[core]["y"]
    return out
```

```python
import contextlib
import numpy as np
import concourse.bass as bass
import concourse.mybir as mybir
from concourse.bass_utils import run_bass_kernel_spmd

F32 = mybir.dt.float32
BF16 = mybir.dt.bfloat16
AF = mybir.ActivationFunctionType
ALU = mybir.AluOpType
AX = mybir.AxisListType

D = 1024
T = 2176
NTILE = 17
NKEY = 2 * T
NKC = 34
IN_DIM = 1792
EPS = 1e-6
BLOCKS = [(0, 128)] + [(128 + 512 * i, 512) for i in range(4)]
SEM_LIMIT = 30000


class Buf:
    __slots__ = ("w", "r")

    def __init__(self):
        self.w = None
        self.r = {}


class KB:
    def __init__(self, nc, es):
        self.nc = nc
        self.es = es
        self.E = {"pe": nc.tensor, "act": nc.scalar, "dve": nc.vector, "pool": nc.gpsimd, "sp": nc.sync}
        self.sem = {}
        self.cnt = {}
        self.nsem = 0
        for e in self.E:
            self._newsem(e)
        self.seen = {e: {} for e in self.E}
        self.slots = {}

    def _mk(self, name):
        self.nsem += 1
        return self.es.enter_context(self.nc.semaphore(f"{name}_{self.nsem}"))

    def _newsem(self, e):
        self.sem[e] = self._mk("s" + e)
        self.cnt[e] = 0

    def _wait(self, eng, deps):
        best = {}
        for tok in deps:
            if tok is None:
                continue
            s, v, src = tok
            if src == "pe" and eng == "pe":
                continue
            key = id(s)
            if self.seen[eng].get(key, 0) >= v:
                continue
            if key not in best or best[key][1] < v:
                best[key] = (s, v)
        for key, (s, v) in best.items():
            self.E[eng].wait_ge(s, v)
            self.seen[eng][key] = v

    def _signal(self, eng, ins):
        if self.cnt[eng] >= SEM_LIMIT:
            self._newsem(eng)
        self.cnt[eng] += 1
        ins.then_inc(self.sem[eng], 1)
        return (self.sem[eng], self.cnt[eng], eng)

    def _deps(self, reads, writes, extra):
        deps = list(extra)
        for b in reads:
            deps.append(b.w)
        for b in writes:
            deps.append(b.w)
            deps.extend(b.r.values())
        return deps

    def op(self, eng, fn, reads=(), writes=(), extra=()):
        self._wait(eng, self._deps(reads, writes, extra))
        tok = self._signal(eng, fn(self.E[eng]))
        for b in reads:
            b.r[eng] = tok
        for b in writes:
            b.w = tok
            b.r = {}
        return tok

    def group(self, eng, fns, reads=(), writes=(), extra=()):
        self._wait(eng, self._deps(reads, writes, extra))
        ins = None
        for fn in fns:
            ins = fn(self.E[eng])
        tok = self._signal(eng, ins)
        for b in reads:
            b.r[eng] = tok
        for b in writes:
            b.w = tok
            b.r = {}
        return tok

    def dma(self, q, slot, out, in_, reads=(), writes=(), extra=()):
        self._wait(q, self._deps(reads, writes, extra))
        if slot not in self.slots or self.slots[slot][1] >= SEM_LIMIT:
            self.slots[slot] = [self._mk("d" + slot), 0]
        st = self.slots[slot]
        st[1] += 16
        self.E[q].dma_start(out=out, in_=in_).then_inc(st[0], 16)
        tok = (st[0], st[1], "dma")
        for b in reads:
            b.r["dma_" + slot] = tok
        for b in writes:
            b.w = tok
            b.r = {}
        return tok

    def now(self, engs=("pe", "act", "dve")):
        return [(self.sem[e], self.cnt[e], e) for e in engs if self.cnt[e] > 0]

    def barrier(self, engs=("pe", "act", "dve")):
        toks = self.now(engs)
        for e in engs:
            self._wait(e, [t for t in toks if t[2] != e])

    def fence(self, eng, engs=("pe", "act", "dve")):
        self._wait(eng, self.now(engs))


class Rec:
    def __init__(self):
        self.items = []

    def op(self, *a, **kw):
        self.items.append(("op", a, kw))

    def group(self, *a, **kw):
        self.items.append(("group", a, kw))

    def dma(self, *a, **kw):
        self.items.append(("dma", a, kw))


class Switch:
    def __init__(self, real):
        self.real = real
        self.t = real

    def op(self, *a, **kw):
        return self.t.op(*a, **kw)

    def group(self, *a, **kw):
        return self.t.group(*a, **kw)

    def dma(self, *a, **kw):
        return self.t.dma(*a, **kw)

    def __getattr__(self, n):
        return getattr(self.real, n)


def _build(NL, STOP=9):
    nc = bass.Bass("TRN2", target_bir_lowering=False)
    dt_in = lambda n, s: nc.dram_tensor(n, s, F32, kind="ExternalInput").ap()
    xin = dt_in("xin", [T, D])
    cvec = dt_in("cvec", [2, D])
    rope = dt_in("rope", [T, 96])
    ident_d = dt_in("ident", [128, 128])
    w_mod = dt_in("w_mod", [4, D, 6 * D])
    b_mod = dt_in("b_mod", [4, 6 * D])
    norm1_g = dt_in("norm1_g", [4, D])
    w_in = dt_in("w_in", [4, D, IN_DIM])
    q_norm_g = dt_in("q_norm_g", [4, 64])
    k_norm_g = dt_in("k_norm_g", [4, 64])
    gmlp_norm_g = dt_in("gmlp_norm_g", [4, 512])
    w_spatial = dt_in("w_spatial", [4, 4, 128, 128])
    b_spatial = dt_in("b_spatial", [4, 4, 128])
    w_out = dt_in("w_out", [4, D, D])
    norm2_g = dt_in("norm2_g", [4, D])
    w_ff1 = dt_in("w_ff1", [4, D, 4 * D])
    w_ff2 = dt_in("w_ff2", [4, 4 * D, D])
    y = nc.dram_tensor("y", [2048, D], F32, kind="ExternalOutput").ap()
    ib = nc.dram_tensor("ib", [256, T], BF16)
    ob = nc.dram_tensor("ob", [512, T], BF16)

    es = contextlib.ExitStack()
    with es:
        sb = lambda n, s, d=F32: es.enter_context(nc.sbuf_tensor(n, s, d))
        XT = sb("XT", [128, 8, T])
        GM = sb("GM", [128, 4, T], BF16)
        RAO = sb("RAO", [128, 4 * T], BF16)
        R1 = sb("R1", [128, 6 * T + NKC * 130], BF16)
        WA = sb("WA", [128, 8 * IN_DIM], BF16)
        WB = sb("WB", [128, 8192], BF16)
        IDENT = sb("IDENT", [128, 128])
        IDENTB = sb("IDENTB", [128, 128], BF16)
        ONES = sb("ONES", [128, 128])
        MODT = sb("MODT", [128, 48, 2])
        MODN = sb("MODN", [128, 48, 2])
        PPV = sb("PPV", [128, 72])
        ROWS = sb("ROWS", [128, 128])
        GB = sb("GB", [128, 640])
        WST = sb("WST", [128, 4, 128], BF16)
        SC = sb("SC", [128, 8, 2], BF16)
        GE1 = sb("GE1", [128, 8, 2])
        GE2 = sb("GE2", [128, 8, 2])
        W512 = [sb(f"W512_{i}", [128, 512]) for i in range(7)]
        sq0, sq1, rstd, tmp0, tmp1, SQ, T1 = W512
        QR = sb("QR", [128, 512], BF16)
        VN = sb("VN", [128, 512], BF16)
        SQk = sb("SQk", [128, 128])
        T1k = sb("T1k", [128, 128])
        PPk = sb("PPk", [128, 128])
        KR = sb("KR", [128, 128], BF16)
        SM = sb("SM", [128, 64])
        RT = [sb(f"RT{i}", [128, 96]) for i in range(2)]
        PS = [es.enter_context(nc.psum_tensor(f"ps{i}", [128, 512], F32)) for i in range(3)]
        PSS = es.enter_context(nc.psum_tensor("pss", [128, 2048], F32))
        PS = PS + [PSS[:, 512 * i:512 * (i + 1)] for i in range(4)]
        PSB = es.enter_context(nc.psum_tensor("psb", [128, 1024], BF16))

        k = Switch(KB(nc, es))
        bSMq, bSMk, bSMg = Buf(), Buf(), Buf()
        win_loaded = set()
        bPS = [Buf() for _ in range(7)]
        bPSB = Buf()
        bW512 = [Buf() for _ in range(7)]
        b_sq0, b_sq1, b_rstd, b_tmp0, b_tmp1, b_SQ, b_T1 = bW512
        bQR, bVN, bSQk, bT1k, bPPk, bKR, bSM = (Buf() for _ in range(7))
        bRT = [Buf(), Buf()]
        bXT = [[Buf() for _ in BLOCKS] for _ in range(8)]
        bWA, bWB = Buf(), Buf()
        bC = Buf()
        bQT = [Buf() for _ in range(NTILE)]
        bGM = [Buf() for _ in range(NTILE)]
        bKVL = bWB
        bKT, bVA = Buf(), Buf()
        bAO = [[[Buf() for _ in BLOCKS] for _ in range(2)] for _ in range(4)]
        bH2 = [[Buf() for _ in BLOCKS] for _ in range(8)]
        bHT = [Buf() for _ in range(8)]
        bGU = Buf()
        bST = Buf()
        bIB, bOB = Buf(), Buf()

        QT = R1[:, 0:4 * T].rearrange("p (j t) -> p j t", j=4)
        KT = R1[:, 4 * T:6 * T]
        VA = R1[:, 6 * T:6 * T + NKC * 130].rearrange("p (c d) -> p c d", d=130)
        H2T = R1[:, 0:8 * T].rearrange("p (k t) -> p k t", k=8)
        KVL = WB[:, 0:2 * T].rearrange("p (a t) -> p a t", a=2)
        WSR = QR[:, :].rearrange("p (g q) -> p g q", g=4)
        GROW0, GROW1 = sq0, sq1
        AO = RAO[:, :].rearrange("p (j t) -> p j t", j=4)
        HTb = RAO[:, 0:4096].rearrange("p (k t) -> p k t", k=8)
        GU = RAO[:, 4096:8192].rearrange("p (g t) -> p g t", g=4)
        WM = [RAO[:, 0:4096].rearrange("p (k n) -> p k n", k=8),
              RAO[:, 4096:8192].rearrange("p (k n) -> p k n", k=8)]
        bWM = [Buf(), Buf()]
        bMODN = Buf()
        WMB = [WA[:, 4096 + 4096 * i:8192 + 4096 * i].rearrange("p (k n) -> p k n", k=8) for i in range(2)]
        bWMB = [Buf(), Buf()]
        AF2 = [RAO[:, 0:2048].rearrange("p (j t) -> p j t", j=4),
               RAO[:, 2048:4096].rearrange("p (j t) -> p j t", j=4)]
        bAF = [Buf(), Buf()]
        WIN = WA[:, :].rearrange("p (k n) -> p k n", k=8)
        PT = [WA[:, i * 512:(i + 1) * 512] for i in range(3)]
        bPT = [Buf() for _ in range(3)]
        XS = [sq0, sq1]
        WOA = WB[:, 0:4096].rearrange("p (j n) -> p j n", j=4)
        WOG = WB[:, 4096:8192].rearrange("p (g n) -> p g n", g=4)
        FW1 = [WA[:, 0:4096].rearrange("p (k n) -> p k n", k=8), WB[:, 0:4096].rearrange("p (k n) -> p k n", k=8)]
        FW2 = [WA[:, 4096:8192].rearrange("p (j n) -> p j n", j=4), WB[:, 4096:8192].rearrange("p (j n) -> p j n", j=4)]
        bFW = [bWA, bWB]

        k.dma("sp", "c0", IDENT[:, :], ident_d[:, :], writes=[bC])
        k.op("dve", lambda e: e.memset(ONES[:, :], 1.0), writes=[bC])
        k.op("act", lambda e: e.activation(out=IDENTB[:, :], in_=IDENT[:, :], func=AF.Copy), reads=[bC], writes=[bST])
        CV = ROWS
        k.dma("sp", "c1", sq0[0:2, :], cvec[:, 0:512], writes=[b_sq0])
        k.dma("sp", "c1b", sq1[0:2, :], cvec[:, 512:1024], writes=[b_sq1])
        k.op("act", lambda e: e.activation(out=tmp0[0:2, :], in_=sq0[0:2, :], func=AF.Silu), reads=[b_sq0], writes=[b_tmp0])
        k.op("act", lambda e: e.activation(out=tmp1[0:2, :], in_=sq1[0:2, :], func=AF.Silu), reads=[b_sq1], writes=[b_tmp1])
        fns = []
        for kk in range(8):
            src = (tmp0 if kk < 4 else tmp1)
            c0 = (kk % 4) * 128
            fns.append(lambda e, kk=kk, src=src, c0=c0: e.matmul(PS[0][:, 2 * kk:2 * kk + 2], src[0:2, c0:c0 + 128], IDENT[0:2, 0:2], start=True, stop=True))
        k.group("pe", fns, reads=[b_tmp0, b_tmp1, bC], writes=[bPS[0]])
        k.op("dve", lambda e: e.tensor_copy(out=SC[:, :, :], in_=PS[0][:, 0:16].rearrange("p (k s) -> p k s", s=2)), reads=[bPS[0]], writes=[bST])

        XSt = [SQ, T1]
        bXS = [b_SQ, b_T1]
        for tt in range(NTILE):
            s = tt % 2
            k.dma("sp", f"xs{s}", XSt[s][:, :], xin[tt * 128:(tt + 1) * 128, 0:512], writes=[bXS[s]])
            bi = 0 if tt == 0 else 1 + (tt - 1) // 4
            for hf in range(2):
                if hf == 1:
                    k.dma("sp", f"xs{s}", XSt[s][:, :], xin[tt * 128:(tt + 1) * 128, 512:1024], writes=[bXS[s]])
                pb = (2 * tt + hf) % 4
                fns = [lambda e, j=j, s=s, pb=pb: e.transpose(PS[pb][:, j * 128:(j + 1) * 128], XSt[s][:, j * 128:(j + 1) * 128], IDENT[:, :]) for j in range(4)]
                k.group("pe", fns, reads=[bXS[s], bC], writes=[bPS[pb]])
                eng = "act" if hf == 0 else "dve"
                dst = XT[:, hf * 4:(hf + 1) * 4, tt * 128:(tt + 1) * 128]
                srcv = PS[pb][:, :].rearrange("p (j t) -> p j t", j=4)
                if eng == "act":
                    k.op("act", lambda e, dst=dst, srcv=srcv: e.activation(out=dst, in_=srcv, func=AF.Copy),
                         reads=[bPS[pb]], writes=[bXT[kk][bi] for kk in range(hf * 4, hf * 4 + 4)])
                else:
                    k.op("dve", lambda e, dst=dst, srcv=srcv: e.tensor_copy(out=dst, in_=srcv),
                         reads=[bPS[pb]], writes=[bXT[kk][bi] for kk in range(hf * 4, hf * 4 + 4)])

        def norm_block(bi, GE, sh_base, dst_fn, dst_bufs_fn):
            bs, bn = BLOCKS[bi]
            sidx = 1 if bi == 0 else 0
            sqs, bsq = [sq0, sq1], [b_sq0, b_sq1]
            for kk in range(8):
                s = kk % 2
                k.op("act", lambda e, kk=kk, s=s: e.activation(out=sqs[s][:, :bn], in_=XT[:, kk, bs:bs + bn], func=AF.Square),
                     reads=[bXT[kk][bi]], writes=[bsq[s]])
                k.op("pe", lambda e, kk=kk, s=s: e.matmul(PS[0][:, :bn], ONES[:, :], sqs[s][:, :bn], start=(kk == 0), stop=(kk == 7)),
                     reads=[bsq[s]], writes=[bPS[0]])
            k.op("act", lambda e: e.activation(out=rstd[:, :bn], in_=PS[0][:, :bn], func=AF.Sqrt, scale=1.0 / D, bias=EPS),
                 reads=[bPS[0]], writes=[b_rstd])
            k.op("dve", lambda e: e.reciprocal(out=rstd[:, :bn], in_=rstd[:, :bn]), reads=[b_rstd], writes=[b_rstd])
            tms, btm = [tmp0, tmp1], [b_tmp0, b_tmp1]
            for kk in range(8):
                s = kk % 2
                k.op("dve", lambda e, kk=kk, s=s: e.scalar_tensor_tensor(out=tms[s][:, :bn], in0=XT[:, kk, bs:bs + bn],
                                                                         scalar=GE[:, kk, sidx:sidx + 1], in1=rstd[:, :bn],
                                                                         op0=ALU.mult, op1=ALU.mult),
                     reads=[bXT[kk][bi], b_rstd, bC], writes=[btm[s]])
                k.op("act", lambda e, kk=kk, s=s: e.activation(out=dst_fn(kk), in_=tms[s][:, :bn], func=AF.Identity,
                                                               bias=MODT[:, sh_base + kk, sidx:sidx + 1], scale=1.0),
                     reads=[btm[s], bC], writes=dst_bufs_fn(kk))

        def rms_rope(PSsrc, bps, ncol, nh, SQt, bSQt, T1t, bT1t, PPt, bPPt, GBv, rt, brt, out_ap_fn, bout, ss_off, bSM):
            SS = SM[:, ss_off:ss_off + nh]
            k.op("act", lambda e: e.activation(out=SQt[:, :ncol], in_=PSsrc, func=AF.Square), reads=[bps], writes=[bSQt])
            k.op("dve", lambda e: e.tensor_reduce(out=SS, in_=SQt[:, :ncol].rearrange("p (h d) -> p h d", h=nh), axis=AX.X, op=ALU.add),
                 reads=[bSQt], writes=[bSM])
            k.op("act", lambda e: e.activation(out=SS, in_=SS, func=AF.Sqrt, scale=1.0 / 64, bias=EPS), reads=[bSM], writes=[bSM])
            k.op("dve", lambda e: e.reciprocal(out=SS, in_=SS), reads=[bSM], writes=[bSM])
            k.op("dve", lambda e: e.tensor_tensor(out=T1t[:, :ncol].rearrange("p (h d) -> p h d", h=nh),
                                                  in0=PSsrc.rearrange("p (h d) -> p h d", h=nh),
                                                  in1=SS.unsqueeze(2).broadcast_to([128, nh, 64]), op=ALU.mult),
                 reads=[bps, bSM], writes=[bT1t])
            k.op("dve", lambda e: e.tensor_tensor(out=T1t[:, :ncol].rearrange("p (h d) -> p h d", h=nh),
                                                  in0=T1t[:, :ncol].rearrange("p (h d) -> p h d", h=nh),
                                                  in1=GBv.unsqueeze(1).broadcast_to([128, nh, 64]), op=ALU.mult),
                 reads=[bT1t, bC], writes=[bT1t])
            v5 = lambda t: t[:, :ncol].rearrange("p (h a f d) -> p h a f d", h=nh, a=2, f=2)
            cosb = rt[:, 0:32].rearrange("p (a d) -> p a d", a=2).unsqueeze(1).broadcast_to([128, nh, 2, 16])
            sinb = rt[:, 32:64].rearrange("p (a d) -> p a d", a=2).unsqueeze(1).broadcast_to([128, nh, 2, 16])
            nsinb = rt[:, 64:96].rearrange("p (a d) -> p a d", a=2).unsqueeze(1).broadcast_to([128, nh, 2, 16])
            k.op("dve", lambda e: e.tensor_tensor(out=v5(PPt)[:, :, :, 0, :], in0=v5(T1t)[:, :, :, 1, :], in1=nsinb, op=ALU.mult),
                 reads=[bT1t, brt], writes=[bPPt])
            k.op("dve", lambda e: e.tensor_tensor(out=v5(PPt)[:, :, :, 1, :], in0=v5(T1t)[:, :, :, 0, :], in1=sinb, op=ALU.mult),
                 reads=[bT1t, brt], writes=[bPPt])
            for f in range(2):
                k.op("dve", lambda e, f=f: e.tensor_tensor(out=v5(T1t)[:, :, :, f, :], in0=v5(T1t)[:, :, :, f, :], in1=cosb, op=ALU.mult),
                     reads=[bPPt, brt], writes=[bT1t])
            o, a, b = out_ap_fn(T1t, PPt)
            k.op("dve", lambda e: e.tensor_tensor(out=o, in0=a, in1=b, op=ALU.add), reads=[bT1t, bPPt], writes=[bout])

        for l in range(NL):
            k.barrier()
            if STOP < 1:
                continue
            k.fence("sp")
            k.fence("pool")
            k.dma("sp", "c2", ROWS[0:8, :], norm1_g[l].rearrange("(k p) -> k p", p=128), writes=[bC])
            k.dma("sp", "c2", ROWS[8:16, :], norm2_g[l].rearrange("(k p) -> k p", p=128), writes=[bC])
            k.dma("sp", "c2", ROWS[16:20, :], gmlp_norm_g[l].rearrange("(k p) -> k p", p=128), writes=[bC])
            k.dma("sp", "c2", ROWS[20:68, :], b_mod[l].rearrange("(k p) -> k p", p=128), writes=[bC])
            k.dma("sp", "c2b", GROW0[0:1, 0:64], q_norm_g[l:l + 1, :], writes=[b_sq0])
            k.dma("sp", "c2b", GROW0[0:1, 64:128], k_norm_g[l:l + 1, :], writes=[b_sq0])
            k.dma("sp", "c2b", GROW0[0:1, 128:512], b_spatial[l:l + 1, 0:3, :].rearrange("o g p -> o (g p)"), writes=[b_sq0])
            k.dma("sp", "c2c", GROW1[0:1, 0:128], b_spatial[l:l + 1, 3, :], writes=[b_sq1])
            k.dma("pool", "c3", WSR[:, :, :], w_spatial[l].rearrange("g p q -> p g q"), writes=[bQR])
            k.op("pe", lambda e: e.matmul(PS[1][:, 0:68], ROWS[0:68, :], IDENT[0:68, 0:68], start=True, stop=True), reads=[bC], writes=[bPS[1]])
            k.op("dve", lambda e: e.tensor_copy(out=PPV[:, 0:68], in_=PS[1][:, 0:68]), reads=[bPS[1]], writes=[bC])
            k.group("pe", [lambda e: e.matmul(PS[2][:, 0:512], ONES[0:1, :], GROW0[0:1, 0:512], start=True, stop=True),
                           lambda e: e.matmul(PS[3][:, 0:128], ONES[0:1, :], GROW1[0:1, 0:128], start=True, stop=True)],
                    reads=[bC, b_sq0, b_sq1], writes=[bPS[2], bPS[3]])
            k.op("dve", lambda e: e.tensor_copy(out=GB[:, 0:512], in_=PS[2][:, 0:512]), reads=[bPS[2]], writes=[bC])
            k.op("dve", lambda e: e.tensor_copy(out=GB[:, 512:640], in_=PS[3][:, 0:128]), reads=[bPS[3]], writes=[bC])
            k.op("dve", lambda e: e.tensor_scalar(out=GB[:, 0:64], in0=GB[:, 0:64], scalar1=0.125, scalar2=None, op0=ALU.mult), reads=[bC], writes=[bC])
            k.group("pe", [lambda e, g=g: e.transpose(PSB[:, g * 128:(g + 1) * 128], WSR[:, g, :], IDENTB[:, :]) for g in range(4)],
                    reads=[bQR, bST], writes=[bPSB])
            k.op("act", lambda e: e.activation(out=WST[:, :, :], in_=PSB[:, 0:512].rearrange("p (g q) -> p g q", g=4), func=AF.Copy),
                 reads=[bPSB], writes=[bC])
            def mod_dma(ln, g, bufs, bbufs, tag):
                s_ = g % 2
                for kk in range(8):
                    k.dma("pool", f"{tag}{s_}", bufs[s_][:, kk, :], w_mod[ln, kk * 128:(kk + 1) * 128, g * 512:(g + 1) * 512], writes=[bbufs[s_]])

            def mod_mm(g, bufs, bbufs, pb, extra_reads):
                s_ = g % 2
                fns = []
                for c4 in range(4):
                    for kk in range(8):
                        fns.append(lambda e, c4=c4, kk=kk: e.matmul(PS[pb][:, 2 * c4:2 * c4 + 2], bufs[s_][:, kk, c4 * 128:(c4 + 1) * 128],
                                                                    SC[:, kk, :], start=(kk == 0), stop=(kk == 7)))
                k.group("pe", fns, reads=[bbufs[s_], bST] + extra_reads, writes=[bPS[pb]])
                k.op("dve", lambda e: e.tensor_copy(out=MODN[:, g * 4:(g + 1) * 4, :], in_=PS[pb][:, 0:8].rearrange("p (c s) -> p c s", s=2)),
                     reads=[bPS[pb]], writes=[bMODN])

            if l == 0:
                for g in range(12):
                    mod_dma(0, g, WM, bWM, "wm")
                    mod_mm(g, WM, bWM, 4, [])
            k.op("dve", lambda e: e.tensor_tensor(out=MODT[:, :, :], in0=MODN[:, :, :],
                                                  in1=PPV[:, 20:68].unsqueeze(2).broadcast_to([128, 48, 2]), op=ALU.add),
                 reads=[bMODN, bC], writes=[bC])
            k.op("dve", lambda e: e.scalar_tensor_tensor(out=GE1[:, :, :], in0=MODT[:, 8:16, :], scalar=1.0,
                                                         in1=PPV[:, 0:8].unsqueeze(2).broadcast_to([128, 8, 2]), op0=ALU.add, op1=ALU.mult),
                 reads=[bC], writes=[bC])
            k.op("dve", lambda e: e.scalar_tensor_tensor(out=GE2[:, :, :], in0=MODT[:, 32:40, :], scalar=1.0,
                                                         in1=PPV[:, 8:16].unsqueeze(2).broadcast_to([128, 8, 2]), op0=ALU.add, op1=ALU.mult),
                 reads=[bC], writes=[bC])
            k.barrier()

            if STOP < 2:
                continue
            if l not in win_loaded:
                k.fence("pool")
                for kk in range(8):
                    k.dma("pool", "win", WIN[:, kk, :], w_in[l, kk * 128:(kk + 1) * 128, :], writes=[bWA])

            for bi, (bs, bn) in enumerate(BLOCKS):
                sidx = 1 if bi == 0 else 0
                norm_block(bi, GE1, 0, lambda kk: HTb[:, kk, :bn], lambda kk: [bHT[kk]])
                for uc in range(4):
                    pb = 5 + (uc % 2)
                    fns = [lambda e, kk=kk, uc=uc, pb=pb: e.matmul(PS[pb][:, :bn], WIN[:, kk, 768 + uc * 128:768 + (uc + 1) * 128], HTb[:, kk, :bn],
                                                                 start=(kk == 0), stop=(kk == 7)) for kk in range(8)]
                    k.group("pe", fns, reads=bHT + [bWA], writes=[bPS[pb]])
                    k.op("act", lambda e, uc=uc, pb=pb: e.activation(out=GU[:, uc, :bn], in_=PS[pb][:, :bn], func=AF.Gelu_apprx_tanh),
                         reads=[bPS[pb]], writes=[bGU])
                for tt in range(bn // 128):
                    gt = bs // 128 + tt
                    tsl = slice(tt * 128, (tt + 1) * 128)
                    gsl = slice(gt * 128, (gt + 1) * 128)
                    r = gt % 2
                    k.dma("sp", f"rt{r}", RT[r][:, :], rope[gt * 128:(gt + 1) * 128, :], writes=[bRT[r]])
                    for (pb, c0, ncol) in ((1, 0, 512), (2, 512, 256), (3, 1280, 512)):
                        fns = [lambda e, kk=kk, pb=pb, c0=c0, ncol=ncol: e.matmul(PS[pb][:, :ncol], HTb[:, kk, tsl], WIN[:, kk, c0:c0 + ncol],
                                                                                start=(kk == 0), stop=(kk == 7)) for kk in range(8)]
                        k.group("pe", fns, reads=bHT + [bWA], writes=[bPS[pb]])
                    rq = Rec()
                    k.t = rq
                    rms_rope(PS[1][:, 0:512], bPS[1], 512, 8, SQ, b_SQ, T1, b_T1, tmp0, b_tmp0, GB[:, 0:64], RT[r], bRT[r],
                             lambda A, B: (QR[:, :].rearrange("p (j s d) -> p s j d", j=4, s=2),
                                           A[:, :].rearrange("p (s j d) -> p s j d", s=2, j=4),
                                           B[:, :].rearrange("p (s j d) -> p s j d", s=2, j=4)), bQR, 0, bSMq)
                    k.group("pe", [lambda e, j=j: e.transpose(PSB[:, j * 128:(j + 1) * 128], QR[:, j * 128:(j + 1) * 128], IDENTB[:, :]) for j in range(4)],
                            reads=[bQR, bST], writes=[bPSB])
                    k.op("act", lambda e, gsl=gsl: e.activation(out=QT[:, :, gsl], in_=PSB[:, 0:512].rearrange("p (j t) -> p j t", j=4), func=AF.Copy),
                         reads=[bPSB], writes=[bQT[gt]])
                    rk = Rec()
                    k.t = rk
                    rms_rope(PS[2][:, 0:128], bPS[2], 128, 2, SQk, bSQk, T1k, bT1k, PPk, bPPk, GB[:, 64:128], RT[r], bRT[r],
                             lambda A, B: (KR[:, :], A[:, :], B[:, :]), bKR, 8, bSMk)
                    k.op("pe", lambda e: e.transpose(PSB[:, 512:640], KR[:, :], IDENTB[:, :]), reads=[bKR, bST], writes=[bPSB])
                    k.op("act", lambda e, gsl=gsl: e.activation(out=KVL[:, 0, gsl], in_=PSB[:, 512:640], func=AF.Copy), reads=[bPSB], writes=[bKVL])
                    k.op("act", lambda e, gsl=gsl: e.activation(out=KVL[:, 1, gsl], in_=PS[2][:, 128:256], func=AF.Copy), reads=[bPS[2]], writes=[bKVL])
                    rg = Rec()
                    k.t = rg
                    bSM = bSMg
                    GG, bGG, TT, bTT = sq0, b_sq0, sq1, b_sq1
                    k.op("act", lambda e: e.activation(out=GG[:, :], in_=PS[3][:, :], func=AF.Gelu_apprx_tanh), reads=[bPS[3]], writes=[bGG])
                    S1, S2, MEAN, MSQ = SM[:, 16:20], SM[:, 20:24], SM[:, 24:28], SM[:, 28:32]
                    k.op("dve", lambda e: e.tensor_reduce(out=S1, in_=GG[:, :].rearrange("p (g c) -> p g c", g=4), axis=AX.X, op=ALU.add), reads=[bGG], writes=[bSM])
                    k.op("act", lambda e: e.activation(out=TT[:, :], in_=GG[:, :], func=AF.Square), reads=[bGG], writes=[bTT])
                    k.op("dve", lambda e: e.tensor_reduce(out=S2, in_=TT[:, :].rearrange("p (g c) -> p g c", g=4), axis=AX.X, op=ALU.add), reads=[bTT], writes=[bSM])
                    k.op("dve", lambda e: e.tensor_scalar(out=MEAN, in0=S1, scalar1=1.0 / 128, scalar2=None, op0=ALU.mult), reads=[bSM], writes=[bSM])
                    k.op("dve", lambda e: e.tensor_tensor(out=MSQ, in0=MEAN, in1=MEAN, op=ALU.mult), reads=[bSM], writes=[bSM])
                    k.op("dve", lambda e: e.scalar_tensor_tensor(out=S2, in0=S2, scalar=1.0 / 128, in1=MSQ, op0=ALU.mult, op1=ALU.subtract), reads=[bSM], writes=[bSM])
                    k.op("act", lambda e: e.activation(out=S2, in_=S2, func=AF.Sqrt, scale=1.0, bias=EPS), reads=[bSM], writes=[bSM])
                    k.op("dve", lambda e: e.reciprocal(out=S2, in_=S2), reads=[bSM], writes=[bSM])
                    g3 = lambda t: t[:, :].rearrange("p (g c) -> p g c", g=4)
                    k.op("dve", lambda e: e.tensor_tensor(out=g3(GG), in0=g3(GG), in1=MEAN.unsqueeze(2).broadcast_to([128, 4, 128]), op=ALU.subtract),
                         reads=[bSM], writes=[bGG])
                    k.op("dve", lambda e: e.tensor_tensor(out=g3(VN), in0=g3(GG), in1=S2.unsqueeze(2).broadcast_to([128, 4, 128]), op=ALU.mult),
                         reads=[bGG, bSM], writes=[bVN])
                    k.group("pe", [lambda e, g=g: e.matmul(PS[4][:, g * 128:(g + 1) * 128], VN[:, g * 128:(g + 1) * 128], WST[:, g, :], start=True, stop=True)
                                   for g in range(4)], reads=[bVN, bC], writes=[bPS[4]])
                    for g in range(4):
                        k.op("dve", lambda e, g=g: e.scalar_tensor_tensor(out=TT[:, g * 128:(g + 1) * 128], in0=PS[4][:, g * 128:(g + 1) * 128],
                                                                          scalar=PPV[:, 16 + g:17 + g], in1=GB[:, 128 + g * 128:128 + (g + 1) * 128],
                                                                          op0=ALU.mult, op1=ALU.add),
                             reads=[bPS[4], bC], writes=[bTT])
                    k.op("dve", lambda e, gsl=gsl, tsl=tsl: e.tensor_tensor(out=GM[:, :, gsl], in0=g3(TT), in1=GU[:, :, tsl], op=ALU.mult),
                         reads=[bTT, bGU], writes=[bGM[gt]])
                    k.t = k.real
                    chains = [rq.items, rk.items, rg.items]
                    pos = [0, 0, 0]
                    while any(pos[i] < len(chains[i]) for i in range(3)):
                        for i in range(3):
                            if pos[i] < len(chains[i]):
                                kind, a_, kw_ = chains[i][pos[i]]
                                pos[i] += 1
                                getattr(k.real, kind)(*a_, **kw_)

            if STOP < 3:
                continue
            k.fence("pool")
            k.dma("pool", "ib", ib[0:128, :], KVL[:, 0, :], reads=[bKVL], writes=[bIB])
            k.dma("pool", "ib", ib[128:256, :], KVL[:, 1, :], reads=[bKVL], writes=[bIB])
            k._wait("pool", [bIB.w, bOB.w] + list(bOB.r.values()))
            tokc = k._signal("pool", nc.gpsimd.collective_compute("AllGather", ALU.bypass, replica_groups=[[0, 1], [2, 3], [4, 5], [6, 7]],
                                                                  ins=[ib.ap().opt()], outs=[ob.ap().opt()]))
            bOB.w = tokc
            bOB.r = {}
            k.barrier()
            k.fence("pool")
            for rk in range(2):
                k.dma("pool", "kt", KT[:, rk * T:(rk + 1) * T], ob[rk * 256:rk * 256 + 128, :], reads=[bOB], writes=[bKT])
                vsrc = ob[rk * 256 + 128:rk * 256 + 256, :].rearrange("p (t d) -> p t d", d=128)
                k.dma("pool", "kv", VA[:, rk * 17:(rk + 1) * 17, 0:64], vsrc[:, :, 0:64], reads=[bOB], writes=[bVA])
                k.dma("pool", "kv", VA[:, rk * 17:(rk + 1) * 17, 65:129], vsrc[:, :, 64:128], reads=[bOB], writes=[bVA])
            k.op("dve", lambda e: e.memset(VA[:, :, 64:65], 1.0), writes=[bVA])
            k.op("dve", lambda e: e.memset(VA[:, :, 129:130], 1.0), writes=[bVA])
            for s2 in range(2):
                for j in range(4):
                    r0 = (s2 * 4 + j) * 64
                    k.dma("pool", "wo", WOA[s2 * 64:(s2 + 1) * 64, j, :], w_out[l, r0:r0 + 64, :], writes=[bWB])
            for g in range(4):
                k.dma("pool", "wo", WOG[:, g, :], w_out[l, 512 + g * 128:512 + (g + 1) * 128, :], writes=[bWB])

            if STOP < 4:
                continue
            RCs, bRCs = [rstd, sq0], [b_rstd, b_sq0]
            BCSs, bBCSs = [tmp0, tmp1], [b_tmp0, b_tmp1]
            PT2 = [WA[:, u * 1024:(u + 1) * 1024].rearrange("p (a n) -> p a n", a=2) for u in range(3)]
            SP2 = [PSS[:, u * 1024:(u + 1) * 1024].rearrange("p (a n) -> p a n", a=2) for u in range(2)]

            def finalize1(bi, bs, bn, j, kvh):
                po = kvh
                RC, bRC = RCs[po], bRCs[po]
                k.op("dve", lambda e: e.reciprocal(out=RC[64:65, :bn], in_=PS[po][64:65, :bn]), reads=[bPS[po]], writes=[bRC])

            def finalize(bi, bs, bn, j, kvh):
                po = kvh
                RC, bRC, BCS, bBCS = RCs[po], bRCs[po], BCSs[po], bBCSs[po]
                k.op("pe", lambda e: e.matmul(PS[2][0:64, :bn], ONES[64:65, 0:64], RC[64:65, :bn], start=True, stop=True), reads=[bRC, bC], writes=[bPS[2]])
                k.op("dve", lambda e: e.tensor_copy(out=BCS[0:64, :bn], in_=PS[2][0:64, :bn]), reads=[bPS[2]], writes=[bBCS])
                if kvh == 0:
                    k.op("dve", lambda e: e.tensor_tensor(out=AO[0:64, j, bs:bs + bn], in0=PS[po][0:64, :bn], in1=BCS[0:64, :bn], op=ALU.mult),
                         reads=[bPS[po], bBCS], writes=[bAO[j][0][bi]])
                else:
                    stg, bstg = ([QR, VN][j % 2], [bQR, bVN][j % 2])
                    k.op("dve", lambda e: e.tensor_tensor(out=stg[0:64, :bn], in0=PS[po][0:64, :bn], in1=BCS[0:64, :bn], op=ALU.mult),
                         reads=[bPS[po], bBCS], writes=[bstg])
                    k.dma("sp", f"ao{j % 2}", AO[64:128, j, bs:bs + bn], stg[0:64, :bn], reads=[bstg], writes=[bAO[j][1][bi]])

            pending = [None]
            for bi, (bs, bn) in enumerate(BLOCKS):
                kcs = [0, 17] if bi == 0 else list(range(NKC))
                qbufs = [bQT[bs // 128 + i] for i in range(bn // 128)]
                n = len(kcs)
                for j in range(4):
                    def S2(u):
                        sp, c = u % 2, kcs[u]
                        fns = [lambda e, a=a: e.matmul(PS[3 + 2 * sp + a][:, :bn], KT[a * 64:(a + 1) * 64, c * 128:(c + 1) * 128],
                                                       QT[a * 64:(a + 1) * 64, j, bs:bs + bn], start=True, stop=True) for a in range(2)]
                        k.group("pe", fns, reads=qbufs + [bKT], writes=[bPS[3 + 2 * sp], bPS[4 + 2 * sp]])

                    def EX2(u):
                        sp, pt = u % 2, u % 3
                        k.op("act", lambda e: e.activation(out=PT2[pt][:, :, :bn], in_=SP2[sp][:, :, :bn], func=AF.Exp),
                             reads=[bPS[3 + 2 * sp], bPS[4 + 2 * sp], bWA], writes=[bPT[pt]])

                    def PV2(u):
                        pt, c = u % 3, kcs[u]
                        fns = [lambda e, a=a: e.matmul(PS[a][0:65, :bn], VA[:, c, a * 65:(a + 1) * 65], PT2[pt][:, a, :bn],
                                                       start=(u == 0), stop=(u == n - 1)) for a in range(2)]
                        k.group("pe", fns, reads=[bPT[pt], bVA], writes=[bPS[0], bPS[1]])

                    S2(0)
                    S2(1)
                    if pending[0] is not None:
                        for a in range(2):
                            finalize1(*pending[0], a)
                        for a in range(2):
                            finalize(*pending[0], a)
                        pending[0] = None
                    for u in range(n):
                        EX2(u)
                        if u + 2 < n:
                            S2(u + 2)
                        PV2(u)
                    pending[0] = (bi, bs, bn, j)
                    if l + 1 < NL and bi >= 1:
                        slot_i = (bi - 1) * 4 + j
                        if slot_i == 0:
                            k.fence("pool")
                        if slot_i < 12:
                            mod_dma(l + 1, slot_i, WMB, bWMB, "wn")
                        if 1 <= slot_i <= 12:
                            mod_mm(slot_i - 1, WMB, bWMB, 2, [bWA])
            for a in range(2):
                finalize1(*pending[0], a)
            for a in range(2):
                finalize(*pending[0], a)
            k.barrier()
            k._wait("pe", [(k.slots[n_][0], k.slots[n_][1], "dma") for n_ in ("ao0", "ao1") if n_ in k.slots])

            if STOP < 5:
                continue
            for bi, (bs, bn) in enumerate(BLOCKS):
                sidx = 1 if bi == 0 else 0
                gmb = [bGM[bs // 128 + i] for i in range(bn // 128)]
                for dch in range(8):
                    pb = dch % 4
                    dsl = slice(dch * 128, (dch + 1) * 128)
                    fns = [lambda e, j=j, pb=pb, dsl=dsl: e.matmul(PS[pb][:, :bn], WOA[:, j, dsl], AO[:, j, bs:bs + bn], start=(j == 0), stop=False) for j in range(4)]
                    fns += [lambda e, g=g, pb=pb, dsl=dsl: e.matmul(PS[pb][:, :bn], WOG[:, g, dsl], GM[:, g, bs:bs + bn], start=False, stop=(g == 3)) for g in range(4)]
                    k.group("pe", fns, reads=[bWB] + gmb + [bAO[j][s2][bi] for j in range(4) for s2 in range(2)], writes=[bPS[pb]])
                    k.op("dve", lambda e, dch=dch, pb=pb: e.scalar_tensor_tensor(out=XT[:, dch, bs:bs + bn], in0=PS[pb][:, :bn], scalar=MODT[:, 16 + dch, sidx:sidx + 1],
                                                                                 in1=XT[:, dch, bs:bs + bn], op0=ALU.mult, op1=ALU.add),
                         reads=[bPS[pb], bC], writes=[bXT[dch][bi]])
            k.barrier()

            if STOP < 6:
                continue
            def load_ffn(g8):
                s = g8 % 2
                for kk in range(8):
                    k.dma("pool", f"fw{s}", FW1[s][:, kk, :], w_ff1[l, kk * 128:(kk + 1) * 128, g8 * 512:(g8 + 1) * 512], writes=[bFW[s]])
                for jj in range(4):
                    k.dma("pool", f"fw{s}", FW2[s][:, jj, :], w_ff2[l, g8 * 512 + jj * 128:g8 * 512 + (jj + 1) * 128, :], writes=[bFW[s]])

            k.fence("pool")
            load_ffn(0)
            for bi, (bs, bn) in enumerate(BLOCKS):
                norm_block(bi, GE2, 24, lambda kk, bs=bs, bn=bn: H2T[:, kk, bs:bs + bn], lambda kk, bi=bi: [bH2[kk][bi]])
            Rt, bRt = [SQ, T1], [b_SQ, b_T1]
            import os
            NG = int(os.environ.get('KNG', '8'))
            units = [(g8, bi) for g8 in range(NG) for bi in range(len(BLOCKS))]

            def a_phase(ui):
                g8, bi = units[ui]
                bs, bn = BLOCKS[bi]
                s, ab = g8 % 2, ui % 2
                for jj in range(4):
                    pb = 1 + (jj % 2)
                    fns = [lambda e, kk=kk, jj=jj, pb=pb: e.matmul(PS[pb][:, :bn], FW1[s][:, kk, jj * 128:(jj + 1) * 128], H2T[:, kk, bs:bs + bn],
                                                                 start=(kk == 0), stop=(kk == 7)) for kk in range(8)]
                    k.group("pe", fns, reads=[bFW[s]] + [bH2[kk][bi] for kk in range(8)], writes=[bPS[pb]])
                    rr = jj % 2
                    k.op("act", lambda e, pb=pb, rr=rr: e.activation(out=Rt[rr][:, :bn], in_=PS[pb][:, :bn], func=AF.Relu), reads=[bPS[pb]], writes=[bRt[rr]])
                    k.op("dve", lambda e, jj=jj, rr=rr: e.tensor_tensor(out=AF2[ab][:, jj, :bn], in0=Rt[rr][:, :bn], in1=Rt[rr][:, :bn], op=ALU.mult),
                         reads=[bRt[rr]], writes=[bAF[ab]])

            def y_phase(ui):
                g8, bi = units[ui]
                bs, bn = BLOCKS[bi]
                s, ab = g8 % 2, ui % 2
                sidx = 1 if bi == 0 else 0
                for dch in range(8):
                    pb = 3 + (dch % 3)
                    dsl = slice(dch * 128, (dch + 1) * 128)
                    fns = [lambda e, jj=jj, pb=pb, dsl=dsl: e.matmul(PS[pb][:, :bn], FW2[s][:, jj, dsl], AF2[ab][:, jj, :bn], start=(jj == 0), stop=(jj == 3)) for jj in range(4)]
                    k.group("pe", fns, reads=[bFW[s], bAF[ab]], writes=[bPS[pb]])
                    k.op("dve", lambda e, dch=dch, pb=pb: e.scalar_tensor_tensor(out=XT[:, dch, bs:bs + bn], in0=PS[pb][:, :bn], scalar=MODT[:, 40 + dch, sidx:sidx + 1],
                                                                                 in1=XT[:, dch, bs:bs + bn], op0=ALU.mult, op1=ALU.add),
                         reads=[bPS[pb], bC], writes=[bXT[dch][bi]])

            if NG > 1:
                load_ffn(1)
            a_phase(0)
            for ui in range(len(units)):
                if ui + 1 < len(units):
                    a_phase(ui + 1)
                y_phase(ui)
                g8, bi = units[ui]
                if bi == len(BLOCKS) - 1:
                    if g8 + 2 < NG:
                        load_ffn(g8 + 2)
                    if g8 == 6 and NG == 8 and l + 1 < NL:
                        for kk in range(8):
                            k.dma("pool", "win", WIN[:, kk, :], w_in[l + 1, kk * 128:(kk + 1) * 128, :], writes=[bWA])
                        win_loaded.add(l + 1)
            k.barrier()

        k.fence("sp")
        OSt, bOS = [SQ, T1], [b_SQ, b_T1]
        ydone = []
        for tt in range(1, NTILE):
            bi = 1 + (tt - 1) // 4
            for hf in range(2):
                s = (2 * tt + hf) % 2
                pb = (2 * tt + hf) % 4
                fns = [lambda e, j=j, hf=hf, pb=pb: e.transpose(PS[pb][:, j * 128:(j + 1) * 128], XT[:, hf * 4 + j, tt * 128:(tt + 1) * 128], IDENT[:, :]) for j in range(4)]
                k.group("pe", fns, reads=[bXT[kk][bi] for kk in range(hf * 4, hf * 4 + 4)] + [bC], writes=[bPS[pb]])
                if hf == 0:
                    k.op("act", lambda e, s=s, pb=pb: e.activation(out=OSt[s][:, :], in_=PS[pb][:, :], func=AF.Copy), reads=[bPS[pb]], writes=[bOS[s]])
                else:
                    k.op("dve", lambda e, s=s, pb=pb: e.tensor_copy(out=OSt[s][:, :], in_=PS[pb][:, :]), reads=[bPS[pb]], writes=[bOS[s]])
                ydone.append(k.dma("sp", f"y{s}", y[(tt - 1) * 128:tt * 128, hf * 512:(hf + 1) * 512], OSt[s][:, :], reads=[bOS[s]]))
        k._wait("sp", ydone)
        k._wait("act", ydone)
    return nc


_NC_CACHE = {}


def _rope_table(half):
    tab = np.zeros((T, 96), np.float32)
    tab[:128, 0:32] = 1.0
    t = (half * 2048 + np.arange(2048)).astype(np.int64)
    row = (t // 64).astype(np.float32)
    col = (t % 64).astype(np.float32)
    inv = (np.float32(10000.0) ** (-np.arange(16, dtype=np.float32) / np.float32(16))).astype(np.float32)
    ang = np.concatenate([row[:, None] * inv[None, :], col[:, None] * inv[None, :]], axis=1).astype(np.float32)
    tab[128:, 0:32] = np.cos(ang)
    tab[128:, 32:64] = np.sin(ang)
    tab[128:, 64:96] = -np.sin(ang)
    return tab


def kernel(x, c, ctx, c_ctx, w_mod, b_mod, norm1_g, w_in, q_norm_g, k_norm_g, gmlp_norm_g,
           w_spatial, b_spatial, w_out, norm2_g, w_ff1, w_ff2, _nl=4, _stop=9):
    f = lambda a: np.ascontiguousarray(np.asarray(a, dtype=np.float32))
    x, c, ctx, c_ctx = f(x), f(c), f(ctx), f(c_ctx)
    shared = dict(w_mod=f(w_mod), b_mod=f(b_mod), norm1_g=f(norm1_g), w_in=f(w_in), q_norm_g=f(q_norm_g),
                  k_norm_g=f(k_norm_g), gmlp_norm_g=f(gmlp_norm_g), w_spatial=f(w_spatial), b_spatial=f(b_spatial),
                  w_out=f(w_out), norm2_g=f(norm2_g), w_ff1=f(w_ff1), w_ff2=f(w_ff2),
                  ident=np.eye(128, dtype=np.float32))
    if (_nl, _stop) not in _NC_CACHE:
        _NC_CACHE[(_nl, _stop)] = _build(_nl, _stop)
    nc = _NC_CACHE[(_nl, _stop)]
    in_maps = []
    for core in range(8):
        b, half = core // 2, core % 2
        xin = np.concatenate([ctx[b, half * 128:(half + 1) * 128], x[b, half * 2048:(half + 1) * 2048]], axis=0)
        m = dict(shared)
        m["xin"] = np.ascontiguousarray(xin)
        m["cvec"] = np.ascontiguousarray(np.stack([c[b], c_ctx], axis=0))
        m["rope"] = _rope_table(half)
        in_maps.append(m)
    res = run_bass_kernel_spmd(nc, in_maps, core_ids=list(range(8)))
    out = np.empty((4, 4096, D), np.float32)
    for core in range(8):
        b, half = core // 2, core % 2
        out[b, half * 2048:(half + 1) * 2048] = res.results[core]["y"]
    return out
```

```python
import contextlib
import numpy as np
import concourse.bass as bass
import concourse.mybir as mybir
from concourse.bass_utils import run_bass_kernel_spmd

F32 = mybir.dt.float32
BF16 = mybir.dt.bfloat16
AF = mybir.ActivationFunctionType
ALU = mybir.AluOpType
AX = mybir.AxisListType

D = 1024
T = 2176
NTILE = 17
NKEY = 2 * T
NKC = 34
IN_DIM = 1792
EPS = 1e-6
BLOCKS = [(0, 128)] + [(128 + 512 * i, 512) for i in range(4)]
SEM_LIMIT = 30000


class Buf:
    __slots__ = ("w", "r")

    def __init__(self):
        self.w = None
        self.r = {}


class KB:
    def __init__(self, nc, es):
        self.nc = nc
        self.es = es
        self.E = {"pe": nc.tensor, "act": nc.scalar, "dve": nc.vector, "pool": nc.gpsimd, "sp": nc.sync}
        self.sem = {}
        self.cnt = {}
        self.nsem = 0
        for e in self.E:
            self._newsem(e)
        self.seen = {e: {} for e in self.E}
        self.slots = {}

    def _mk(self, name):
        self.nsem += 1
        return self.es.enter_context(self.nc.semaphore(f"{name}_{self.nsem}"))

    def _newsem(self, e):
        self.sem[e] = self._mk("s" + e)
        self.cnt[e] = 0

    def _wait(self, eng, deps):
        best = {}
        for tok in deps:
            if tok is None:
                continue
            s, v, src = tok
            if src == "pe" and eng == "pe":
                continue
            key = id(s)
            if self.seen[eng].get(key, 0) >= v:
                continue
            if key not in best or best[key][1] < v:
                best[key] = (s, v)
        for key, (s, v) in best.items():
            self.E[eng].wait_ge(s, v)
            self.seen[eng][key] = v

    def _signal(self, eng, ins):
        if self.cnt[eng] >= SEM_LIMIT:
            self._newsem(eng)
        self.cnt[eng] += 1
        ins.then_inc(self.sem[eng], 1)
        return (self.sem[eng], self.cnt[eng], eng)

    def _deps(self, reads, writes, extra):
        deps = list(extra)
        for b in reads:
            deps.append(b.w)
        for b in writes:
            deps.append(b.w)
            deps.extend(b.r.values())
        return deps

    def op(self, eng, fn, reads=(), writes=(), extra=()):
        self._wait(eng, self._deps(reads, writes, extra))
        tok = self._signal(eng, fn(self.E[eng]))
        for b in reads:
            b.r[eng] = tok
        for b in writes:
            b.w = tok
            b.r = {}
        return tok

    def group(self, eng, fns, reads=(), writes=(), extra=()):
        self._wait(eng, self._deps(reads, writes, extra))
        ins = None
        for fn in fns:
            ins = fn(self.E[eng])
        tok = self._signal(eng, ins)
        for b in reads:
            b.r[eng] = tok
        for b in writes:
            b.w = tok
            b.r = {}
        return tok

    def dma(self, q, slot, out, in_, reads=(), writes=(), extra=()):
        self._wait(q, self._deps(reads, writes, extra))
        if slot not in self.slots or self.slots[slot][1] >= SEM_LIMIT:
            self.slots[slot] = [self._mk("d" + slot), 0]
        st = self.slots[slot]
        st[1] += 16
        self.E[q].dma_start(out=out, in_=in_).then_inc(st[0], 16)
        tok = (st[0], st[1], "dma")
        for b in reads:
            b.r["dma_" + slot] = tok
        for b in writes:
            b.w = tok
            b.r = {}
        return tok

    def now(self, engs=("pe", "act", "dve")):
        return [(self.sem[e], self.cnt[e], e) for e in engs if self.cnt[e] > 0]

    def barrier(self, engs=("pe", "act", "dve")):
        toks = self.now(engs)
        for e in engs:
            self._wait(e, [t for t in toks if t[2] != e])

    def fence(self, eng, engs=("pe", "act", "dve")):
        self._wait(eng, self.now(engs))


class Rec:
    def __init__(self):
        self.items = []

    def op(self, *a, **kw):
        self.items.append(("op", a, kw))

    def group(self, *a, **kw):
        self.items.append(("group", a, kw))

    def dma(self, *a, **kw):
        self.items.append(("dma", a, kw))


class Switch:
    def __init__(self, real):
        self.real = real
        self.t = real

    def op(self, *a, **kw):
        return self.t.op(*a, **kw)

    def group(self, *a, **kw):
        return self.t.group(*a, **kw)

    def dma(self, *a, **kw):
        return self.t.dma(*a, **kw)

    def __getattr__(self, n):
        return getattr(self.real, n)


def _build(NL, STOP=9):
    nc = bass.Bass("TRN2", target_bir_lowering=False)
    dt_in = lambda n, s: nc.dram_tensor(n, s, F32, kind="ExternalInput").ap()
    xin = dt_in("xin", [T, D])
    cvec = dt_in("cvec", [2, D])
    rope = dt_in("rope", [T, 96])
    ident_d = dt_in("ident", [128, 128])
    w_mod = dt_in("w_mod", [4, D, 6 * D])
    b_mod = dt_in("b_mod", [4, 6 * D])
    norm1_g = dt_in("norm1_g", [4, D])
    w_in = dt_in("w_in", [4, D, IN_DIM])
    q_norm_g = dt_in("q_norm_g", [4, 64])
    k_norm_g = dt_in("k_norm_g", [4, 64])
    gmlp_norm_g = dt_in("gmlp_norm_g", [4, 512])
    w_spatial = dt_in("w_spatial", [4, 4, 128, 128])
    b_spatial = dt_in("b_spatial", [4, 4, 128])
    w_out = dt_in("w_out", [4, D, D])
    norm2_g = dt_in("norm2_g", [4, D])
    w_ff1 = dt_in("w_ff1", [4, D, 4 * D])
    w_ff2 = dt_in("w_ff2", [4, 4 * D, D])
    y = nc.dram_tensor("y", [2048, D], F32, kind="ExternalOutput").ap()
    ib = nc.dram_tensor("ib", [256, T], BF16)
    ob = nc.dram_tensor("ob", [512, T], BF16)

    es = contextlib.ExitStack()
    with es:
        sb = lambda n, s, d=F32: es.enter_context(nc.sbuf_tensor(n, s, d))
        XT = sb("XT", [128, 8, T])
        GM = sb("GM", [128, 4, T], BF16)
        RAO = sb("RAO", [128, 4 * T], BF16)
        R1 = sb("R1", [128, 6 * T + NKC * 130], BF16)
        WA = sb("WA", [128, 8 * IN_DIM], BF16)
        WB = sb("WB", [128, 8192], BF16)
        IDENT = sb("IDENT", [128, 128])
        IDENTB = sb("IDENTB", [128, 128], BF16)
        ONES = sb("ONES", [128, 128])
        MODT = sb("MODT", [128, 48, 2])
        MODN = sb("MODN", [128, 48, 2])
        PPV = sb("PPV", [128, 72])
        ROWS = sb("ROWS", [128, 128])
        GB = sb("GB", [128, 640])
        WST = sb("WST", [128, 4, 128], BF16)
        SC = sb("SC", [128, 8, 2], BF16)
        GE1 = sb("GE1", [128, 8, 2])
        GE2 = sb("GE2", [128, 8, 2])
        W512 = [sb(f"W512_{i}", [128, 512]) for i in range(7)]
        sq0, sq1, rstd, tmp0, tmp1, SQ, T1 = W512
        QR = sb("QR", [128, 512], BF16)
        VN = sb("VN", [128, 512], BF16)
        SQk = sb("SQk", [128, 128])
        T1k = sb("T1k", [128, 128])
        PPk = sb("PPk", [128, 128])
        KR = sb("KR", [128, 128], BF16)
        SM = sb("SM", [128, 64])
        RT = [sb(f"RT{i}", [128, 96]) for i in range(2)]
        PS = [es.enter_context(nc.psum_tensor(f"ps{i}", [128, 512], F32)) for i in range(3)]
        PSS = es.enter_context(nc.psum_tensor("pss", [128, 2048], F32))
        PS = PS + [PSS[:, 512 * i:512 * (i + 1)] for i in range(4)]
        PSB = es.enter_context(nc.psum_tensor("psb", [128, 1024], BF16))

        k = Switch(KB(nc, es))
        bSMq, bSMk, bSMg = Buf(), Buf(), Buf()
        win_loaded = set()
        bPS = [Buf() for _ in range(7)]
        bPSB = Buf()
        bW512 = [Buf() for _ in range(7)]
        b_sq0, b_sq1, b_rstd, b_tmp0, b_tmp1, b_SQ, b_T1 = bW512
        bQR, bVN, bSQk, bT1k, bPPk, bKR, bSM = (Buf() for _ in range(7))
        bRT = [Buf(), Buf()]
        bXT = [[Buf() for _ in BLOCKS] for _ in range(8)]
        bWA, bWB = Buf(), Buf()
        bC = Buf()
        bQT = [Buf() for _ in range(NTILE)]
        bGM = [Buf() for _ in range(NTILE)]
        bKVL = bWB
        bKT, bVA = Buf(), Buf()
        bAO = [[[Buf() for _ in BLOCKS] for _ in range(2)] for _ in range(4)]
        bH2 = [[Buf() for _ in BLOCKS] for _ in range(8)]
        bHT = [Buf() for _ in range(8)]
        bGU = Buf()
        bST = Buf()
        bIB, bOB = Buf(), Buf()

        QT = R1[:, 0:4 * T].rearrange("p (j t) -> p j t", j=4)
        KT = R1[:, 4 * T:6 * T]
        VA = R1[:, 6 * T:6 * T + NKC * 130].rearrange("p (c d) -> p c d", d=130)
        H2T = R1[:, 0:8 * T].rearrange("p (k t) -> p k t", k=8)
        KVL = WB[:, 0:2 * T].rearrange("p (a t) -> p a t", a=2)
        WSR = QR[:, :].rearrange("p (g q) -> p g q", g=4)
        GROW0, GROW1 = sq0, sq1
        AO = RAO[:, :].rearrange("p (j t) -> p j t", j=4)
        HTb = RAO[:, 0:4096].rearrange("p (k t) -> p k t", k=8)
        GU = RAO[:, 4096:8192].rearrange("p (g t) -> p g t", g=4)
        WM = [RAO[:, 0:4096].rearrange("p (k n) -> p k n", k=8),
              RAO[:, 4096:8192].rearrange("p (k n) -> p k n", k=8)]
        bWM = [Buf(), Buf()]
        bMODN = Buf()
        WMB = [WA[:, 4096 + 4096 * i:8192 + 4096 * i].rearrange("p (k n) -> p k n", k=8) for i in range(2)]
        bWMB = [Buf(), Buf()]
        AF2 = [RAO[:, 0:2048].rearrange("p (j t) -> p j t", j=4),
               RAO[:, 2048:4096].rearrange("p (j t) -> p j t", j=4)]
        bAF = [Buf(), Buf()]
        WIN = WA[:, :].rearrange("p (k n) -> p k n", k=8)
        PT = [WA[:, i * 512:(i + 1) * 512] for i in range(3)]
        bPT = [Buf() for _ in range(3)]
        XS = [sq0, sq1]
        WOA = WB[:, 0:4096].rearrange("p (j n) -> p j n", j=4)
        WOG = WB[:, 4096:8192].rearrange("p (g n) -> p g n", g=4)
        FW1 = [WA[:, 0:4096].rearrange("p (k n) -> p k n", k=8), WB[:, 0:4096].rearrange("p (k n) -> p k n", k=8)]
        FW2 = [WA[:, 4096:8192].rearrange("p (j n) -> p j n", j=4), WB[:, 4096:8192].rearrange("p (j n) -> p j n", j=4)]
        bFW = [bWA, bWB]

        k.dma("sp", "c0", IDENT[:, :], ident_d[:, :], writes=[bC])
        k.op("dve", lambda e: e.memset(ONES[:, :], 1.0), writes=[bC])
        k.op("act", lambda e: e.activation(out=IDENTB[:, :], in_=IDENT[:, :], func=AF.Copy), reads=[bC], writes=[bST])
        CV = ROWS
        k.dma("sp", "c1", sq0[0:2, :], cvec[:, 0:512], writes=[b_sq0])
        k.dma("sp", "c1b", sq1[0:2, :], cvec[:, 512:1024], writes=[b_sq1])
        k.op("act", lambda e: e.activation(out=tmp0[0:2, :], in_=sq0[0:2, :], func=AF.Silu), reads=[b_sq0], writes=[b_tmp0])
        k.op("act", lambda e: e.activation(out=tmp1[0:2, :], in_=sq1[0:2, :], func=AF.Silu), reads=[b_sq1], writes=[b_tmp1])
        fns = []
        for kk in range(8):
            src = (tmp0 if kk < 4 else tmp1)
            c0 = (kk % 4) * 128
            fns.append(lambda e, kk=kk, src=src, c0=c0: e.matmul(PS[0][:, 2 * kk:2 * kk + 2], src[0:2, c0:c0 + 128], IDENT[0:2, 0:2], start=True, stop=True))
        k.group("pe", fns, reads=[b_tmp0, b_tmp1, bC], writes=[bPS[0]])
        k.op("dve", lambda e: e.tensor_copy(out=SC[:, :, :], in_=PS[0][:, 0:16].rearrange("p (k s) -> p k s", s=2)), reads=[bPS[0]], writes=[bST])

        XSt = [SQ, T1]
        bXS = [b_SQ, b_T1]
        for tt in range(NTILE):
            s = tt % 2
            k.dma("sp", f"xs{s}", XSt[s][:, :], xin[tt * 128:(tt + 1) * 128, 0:512], writes=[bXS[s]])
            bi = 0 if tt == 0 else 1 + (tt - 1) // 4
            for hf in range(2):
                if hf == 1:
                    k.dma("sp", f"xs{s}", XSt[s][:, :], xin[tt * 128:(tt + 1) * 128, 512:1024], writes=[bXS[s]])
                pb = (2 * tt + hf) % 4
                fns = [lambda e, j=j, s=s, pb=pb: e.transpose(PS[pb][:, j * 128:(j + 1) * 128], XSt[s][:, j * 128:(j + 1) * 128], IDENT[:, :]) for j in range(4)]
                k.group("pe", fns, reads=[bXS[s], bC], writes=[bPS[pb]])
                eng = "act" if hf == 0 else "dve"
                dst = XT[:, hf * 4:(hf + 1) * 4, tt * 128:(tt + 1) * 128]
                srcv = PS[pb][:, :].rearrange("p (j t) -> p j t", j=4)
                if eng == "act":
                    k.op("act", lambda e, dst=dst, srcv=srcv: e.activation(out=dst, in_=srcv, func=AF.Copy),
                         reads=[bPS[pb]], writes=[bXT[kk][bi] for kk in range(hf * 4, hf * 4 + 4)])
                else:
                    k.op("dve", lambda e, dst=dst, srcv=srcv: e.tensor_copy(out=dst, in_=srcv),
                         reads=[bPS[pb]], writes=[bXT[kk][bi] for kk in range(hf * 4, hf * 4 + 4)])

        def norm_block(bi, GE, sh_base, dst_fn, dst_bufs_fn):
            bs, bn = BLOCKS[bi]
            sidx = 1 if bi == 0 else 0
            sqs, bsq = [sq0, sq1], [b_sq0, b_sq1]
            for kk in range(8):
                s = kk % 2
                k.op("act", lambda e, kk=kk, s=s: e.activation(out=sqs[s][:, :bn], in_=XT[:, kk, bs:bs + bn], func=AF.Square),
                     reads=[bXT[kk][bi]], writes=[bsq[s]])
                k.op("pe", lambda e, kk=kk, s=s: e.matmul(PS[0][:, :bn], ONES[:, :], sqs[s][:, :bn], start=(kk == 0), stop=(kk == 7)),
                     reads=[bsq[s]], writes=[bPS[0]])
            k.op("act", lambda e: e.activation(out=rstd[:, :bn], in_=PS[0][:, :bn], func=AF.Sqrt, scale=1.0 / D, bias=EPS),
                 reads=[bPS[0]], writes=[b_rstd])
            k.op("dve", lambda e: e.reciprocal(out=rstd[:, :bn], in_=rstd[:, :bn]), reads=[b_rstd], writes=[b_rstd])
            tms, btm = [tmp0, tmp1], [b_tmp0, b_tmp1]
            for kk in range(8):
                s = kk % 2
                k.op("dve", lambda e, kk=kk, s=s: e.scalar_tensor_tensor(out=tms[s][:, :bn], in0=XT[:, kk, bs:bs + bn],
                                                                         scalar=GE[:, kk, sidx:sidx + 1], in1=rstd[:, :bn],
                                                                         op0=ALU.mult, op1=ALU.mult),
                     reads=[bXT[kk][bi], b_rstd, bC], writes=[btm[s]])
                k.op("act", lambda e, kk=kk, s=s: e.activation(out=dst_fn(kk), in_=tms[s][:, :bn], func=AF.Identity,
                                                               bias=MODT[:, sh_base + kk, sidx:sidx + 1], scale=1.0),
                     reads=[btm[s], bC], writes=dst_bufs_fn(kk))

        def rms_rope(PSsrc, bps, ncol, nh, SQt, bSQt, T1t, bT1t, PPt, bPPt, GBv, rt, brt, out_ap_fn, bout, ss_off, bSM):
            SS = SM[:, ss_off:ss_off + nh]
            k.op("act", lambda e: e.activation(out=SQt[:, :ncol], in_=PSsrc, func=AF.Square), reads=[bps], writes=[bSQt])
            k.op("dve", lambda e: e.tensor_reduce(out=SS, in_=SQt[:, :ncol].rearrange("p (h d) -> p h d", h=nh), axis=AX.X, op=ALU.add),
                 reads=[bSQt], writes=[bSM])
            k.op("act", lambda e: e.activation(out=SS, in_=SS, func=AF.Sqrt, scale=1.0 / 64, bias=EPS), reads=[bSM], writes=[bSM])
            k.op("dve", lambda e: e.reciprocal(out=SS, in_=SS), reads=[bSM], writes=[bSM])
            k.op("dve", lambda e: e.tensor_tensor(out=T1t[:, :ncol].rearrange("p (h d) -> p h d", h=nh),
                                                  in0=PSsrc.rearrange("p (h d) -> p h d", h=nh),
                                                  in1=SS.unsqueeze(2).broadcast_to([128, nh, 64]), op=ALU.mult),
                 reads=[bps, bSM], writes=[bT1t])
            k.op("dve", lambda e: e.tensor_tensor(out=T1t[:, :ncol].rearrange("p (h d) -> p h d", h=nh),
                                                  in0=T1t[:, :ncol].rearrange("p (h d) -> p h d", h=nh),
                                                  in1=GBv.unsqueeze(1).broadcast_to([128, nh, 64]), op=ALU.mult),
                 reads=[bT1t, bC], writes=[bT1t])
            v5 = lambda t: t[:, :ncol].rearrange("p (h a f d) -> p h a f d", h=nh, a=2, f=2)
            cosb = rt[:, 0:32].rearrange("p (a d) -> p a d", a=2).unsqueeze(1).broadcast_to([128, nh, 2, 16])
            sinb = rt[:, 32:64].rearrange("p (a d) -> p a d", a=2).unsqueeze(1).broadcast_to([128, nh, 2, 16])
            nsinb = rt[:, 64:96].rearrange("p (a d) -> p a d", a=2).unsqueeze(1).broadcast_to([128, nh, 2, 16])
            k.op("dve", lambda e: e.tensor_tensor(out=v5(PPt)[:, :, :, 0, :], in0=v5(T1t)[:, :, :, 1, :], in1=nsinb, op=ALU.mult),
                 reads=[bT1t, brt], writes=[bPPt])
            k.op("dve", lambda e: e.tensor_tensor(out=v5(PPt)[:, :, :, 1, :], in0=v5(T1t)[:, :, :, 0, :], in1=sinb, op=ALU.mult),
                 reads=[bT1t, brt], writes=[bPPt])
            for f in range(2):
                k.op("dve", lambda e, f=f: e.tensor_tensor(out=v5(T1t)[:, :, :, f, :], in0=v5(T1t)[:, :, :, f, :], in1=cosb, op=ALU.mult),
                     reads=[bPPt, brt], writes=[bT1t])
            o, a, b = out_ap_fn(T1t, PPt)
            k.op("dve", lambda e: e.tensor_tensor(out=o, in0=a, in1=b, op=ALU.add), reads=[bT1t, bPPt], writes=[bout])

        for l in range(NL):
            k.barrier()
            if STOP < 1:
                continue
            k.fence("sp")
            k.fence("pool")
            k.dma("sp", "c2", ROWS[0:8, :], norm1_g[l].rearrange("(k p) -> k p", p=128), writes=[bC])
            k.dma("sp", "c2", ROWS[8:16, :], norm2_g[l].rearrange("(k p) -> k p", p=128), writes=[bC])
            k.dma("sp", "c2", ROWS[16:20, :], gmlp_norm_g[l].rearrange("(k p) -> k p", p=128), writes=[bC])
            k.dma("sp", "c2", ROWS[20:68, :], b_mod[l].rearrange("(k p) -> k p", p=128), writes=[bC])
            k.dma("sp", "c2b", GROW0[0:1, 0:64], q_norm_g[l:l + 1, :], writes=[b_sq0])
            k.dma("sp", "c2b", GROW0[0:1, 64:128], k_norm_g[l:l + 1, :], writes=[b_sq0])
            k.dma("sp", "c2b", GROW0[0:1, 128:512], b_spatial[l:l + 1, 0:3, :].rearrange("o g p -> o (g p)"), writes=[b_sq0])
            k.dma("sp", "c2c", GROW1[0:1, 0:128], b_spatial[l:l + 1, 3, :], writes=[b_sq1])
            k.dma("pool", "c3", WSR[:, :, :], w_spatial[l].rearrange("g p q -> p g q"), writes=[bQR])
            k.op("pe", lambda e: e.matmul(PS[1][:, 0:68], ROWS[0:68, :], IDENT[0:68, 0:68], start=True, stop=True), reads=[bC], writes=[bPS[1]])
            k.op("dve", lambda e: e.tensor_copy(out=PPV[:, 0:68], in_=PS[1][:, 0:68]), reads=[bPS[1]], writes=[bC])
            k.group("pe", [lambda e: e.matmul(PS[2][:, 0:512], ONES[0:1, :], GROW0[0:1, 0:512], start=True, stop=True),
                           lambda e: e.matmul(PS[3][:, 0:128], ONES[0:1, :], GROW1[0:1, 0:128], start=True, stop=True)],
                    reads=[bC, b_sq0, b_sq1], writes=[bPS[2], bPS[3]])
            k.op("dve", lambda e: e.tensor_copy(out=GB[:, 0:512], in_=PS[2][:, 0:512]), reads=[bPS[2]], writes=[bC])
            k.op("dve", lambda e: e.tensor_copy(out=GB[:, 512:640], in_=PS[3][:, 0:128]), reads=[bPS[3]], writes=[bC])
            k.op("dve", lambda e: e.tensor_scalar(out=GB[:, 0:64], in0=GB[:, 0:64], scalar1=0.125, scalar2=None, op0=ALU.mult), reads=[bC], writes=[bC])
            k.group("pe", [lambda e, g=g: e.transpose(PSB[:, g * 128:(g + 1) * 128], WSR[:, g, :], IDENTB[:, :]) for g in range(4)],
                    reads=[bQR, bST], writes=[bPSB])
            k.op("act", lambda e: e.activation(out=WST[:, :, :], in_=PSB[:, 0:512].rearrange("p (g q) -> p g q", g=4), func=AF.Copy),
                 reads=[bPSB], writes=[bC])
            def mod_dma(ln, g, bufs, bbufs, tag):
                s_ = g % 2
                for kk in range(8):
                    k.dma("pool", f"{tag}{s_}", bufs[s_][:, kk, :], w_mod[ln, kk * 128:(kk + 1) * 128, g * 512:(g + 1) * 512], writes=[bbufs[s_]])

            def mod_mm(g, bufs, bbufs, pb, extra_reads):
                s_ = g % 2
                fns = []
                for c4 in range(4):
                    for kk in range(8):
                        fns.append(lambda e, c4=c4, kk=kk: e.matmul(PS[pb][:, 2 * c4:2 * c4 + 2], bufs[s_][:, kk, c4 * 128:(c4 + 1) * 128],
                                                                    SC[:, kk, :], start=(kk == 0), stop=(kk == 7)))
                k.group("pe", fns, reads=[bbufs[s_], bST] + extra_reads, writes=[bPS[pb]])
                k.op("dve", lambda e: e.tensor_copy(out=MODN[:, g * 4:(g + 1) * 4, :], in_=PS[pb][:, 0:8].rearrange("p (c s) -> p c s", s=2)),
                     reads=[bPS[pb]], writes=[bMODN])

            if l == 0:
                for g in range(12):
                    mod_dma(0, g, WM, bWM, "wm")
                    mod_mm(g, WM, bWM, 4, [])
            k.op("dve", lambda e: e.tensor_tensor(out=MODT[:, :, :], in0=MODN[:, :, :],
                                                  in1=PPV[:, 20:68].unsqueeze(2).broadcast_to([128, 48, 2]), op=ALU.add),
                 reads=[bMODN, bC], writes=[bC])
            k.op("dve", lambda e: e.scalar_tensor_tensor(out=GE1[:, :, :], in0=MODT[:, 8:16, :], scalar=1.0,
                                                         in1=PPV[:, 0:8].unsqueeze(2).broadcast_to([128, 8, 2]), op0=ALU.add, op1=ALU.mult),
                 reads=[bC], writes=[bC])
            k.op("dve", lambda e: e.scalar_tensor_tensor(out=GE2[:, :, :], in0=MODT[:, 32:40, :], scalar=1.0,
                                                         in1=PPV[:, 8:16].unsqueeze(2).broadcast_to([128, 8, 2]), op0=ALU.add, op1=ALU.mult),
                 reads=[bC], writes=[bC])
            k.barrier()

            if STOP < 2:
                continue
            if l not in win_loaded:
                k.fence("pool")
                for kk in range(8):
                    k.dma("pool", "win", WIN[:, kk, :], w_in[l, kk * 128:(kk + 1) * 128, :], writes=[bWA])

            for bi, (bs, bn) in enumerate(BLOCKS):
                sidx = 1 if bi == 0 else 0
                norm_block(bi, GE1, 0, lambda kk: HTb[:, kk, :bn], lambda kk: [bHT[kk]])
                for uc in range(4):
                    pb = 5 + (uc % 2)
                    fns = [lambda e, kk=kk, uc=uc, pb=pb: e.matmul(PS[pb][:, :bn], WIN[:, kk, 768 + uc * 128:768 + (uc + 1) * 128], HTb[:, kk, :bn],
                                                                 start=(kk == 0), stop=(kk == 7)) for kk in range(8)]
                    k.group("pe", fns, reads=bHT + [bWA], writes=[bPS[pb]])
                    k.op("act", lambda e, uc=uc, pb=pb: e.activation(out=GU[:, uc, :bn], in_=PS[pb][:, :bn], func=AF.Gelu_apprx_tanh),
                         reads=[bPS[pb]], writes=[bGU])
                for tt in range(bn // 128):
                    gt = bs // 128 + tt
                    tsl = slice(tt * 128, (tt + 1) * 128)
                    gsl = slice(gt * 128, (gt + 1) * 128)
                    r = gt % 2
                    k.dma("sp", f"rt{r}", RT[r][:, :], rope[gt * 128:(gt + 1) * 128, :], writes=[bRT[r]])
                    for (pb, c0, ncol) in ((1, 0, 512), (2, 512, 256), (3, 1280, 512)):
                        fns = [lambda e, kk=kk, pb=pb, c0=c0, ncol=ncol: e.matmul(PS[pb][:, :ncol], HTb[:, kk, tsl], WIN[:, kk, c0:c0 + ncol],
                                                                                start=(kk == 0), stop=(kk == 7)) for kk in range(8)]
                        k.group("pe", fns, reads=bHT + [bWA], writes=[bPS[pb]])
                    rq = Rec()
                    k.t = rq
                    rms_rope(PS[1][:, 0:512], bPS[1], 512, 8, SQ, b_SQ, T1, b_T1, tmp0, b_tmp0, GB[:, 0:64], RT[r], bRT[r],
                             lambda A, B: (QR[:, :].rearrange("p (j s d) -> p s j d", j=4, s=2),
                                           A[:, :].rearrange("p (s j d) -> p s j d", s=2, j=4),
                                           B[:, :].rearrange("p (s j d) -> p s j d", s=2, j=4)), bQR, 0, bSMq)
                    k.group("pe", [lambda e, j=j: e.transpose(PSB[:, j * 128:(j + 1) * 128], QR[:, j * 128:(j + 1) * 128], IDENTB[:, :]) for j in range(4)],
                            reads=[bQR, bST], writes=[bPSB])
                    k.op("act", lambda e, gsl=gsl: e.activation(out=QT[:, :, gsl], in_=PSB[:, 0:512].rearrange("p (j t) -> p j t", j=4), func=AF.Copy),
                         reads=[bPSB], writes=[bQT[gt]])
                    rk = Rec()
                    k.t = rk
                    rms_rope(PS[2][:, 0:128], bPS[2], 128, 2, SQk, bSQk, T1k, bT1k, PPk, bPPk, GB[:, 64:128], RT[r], bRT[r],
                             lambda A, B: (KR[:, :], A[:, :], B[:, :]), bKR, 8, bSMk)
                    k.op("pe", lambda e: e.transpose(PSB[:, 512:640], KR[:, :], IDENTB[:, :]), reads=[bKR, bST], writes=[bPSB])
                    k.op("act", lambda e, gsl=gsl: e.activation(out=KVL[:, 0, gsl], in_=PSB[:, 512:640], func=AF.Copy), reads=[bPSB], writes=[bKVL])
                    k.op("act", lambda e, gsl=gsl: e.activation(out=KVL[:, 1, gsl], in_=PS[2][:, 128:256], func=AF.Copy), reads=[bPS[2]], writes=[bKVL])
                    rg = Rec()
                    k.t = rg
                    bSM = bSMg
                    GG, bGG, TT, bTT = sq0, b_sq0, sq1, b_sq1
                    k.op("act", lambda e: e.activation(out=GG[:, :], in_=PS[3][:, :], func=AF.Gelu_apprx_tanh), reads=[bPS[3]], writes=[bGG])
                    S1, S2, MEAN, MSQ = SM[:, 16:20], SM[:, 20:24], SM[:, 24:28], SM[:, 28:32]
                    k.op("dve", lambda e: e.tensor_reduce(out=S1, in_=GG[:, :].rearrange("p (g c) -> p g c", g=4), axis=AX.X, op=ALU.add), reads=[bGG], writes=[bSM])
                    k.op("act", lambda e: e.activation(out=TT[:, :], in_=GG[:, :], func=AF.Square), reads=[bGG], writes=[bTT])
                    k.op("dve", lambda e: e.tensor_reduce(out=S2, in_=TT[:, :].rearrange("p (g c) -> p g c", g=4), axis=AX.X, op=ALU.add), reads=[bTT], writes=[bSM])
                    k.op("dve", lambda e: e.tensor_scalar(out=MEAN, in0=S1, scalar1=1.0 / 128, scalar2=None, op0=ALU.mult), reads=[bSM], writes=[bSM])
                    k.op("dve", lambda e: e.tensor_tensor(out=MSQ, in0=MEAN, in1=MEAN, op=ALU.mult), reads=[bSM], writes=[bSM])
                    k.op("dve", lambda e: e.scalar_tensor_tensor(out=S2, in0=S2, scalar=1.0 / 128, in1=MSQ, op0=ALU.mult, op1=ALU.subtract), reads=[bSM], writes=[bSM])
                    k.op("act", lambda e: e.activation(out=S2, in_=S2, func=AF.Sqrt, scale=1.0, bias=EPS), reads=[bSM], writes=[bSM])
                    k.op("dve", lambda e: e.reciprocal(out=S2, in_=S2), reads=[bSM], writes=[bSM])
                    g3 = lambda t: t[:, :].rearrange("p (g c) -> p g c", g=4)
                    k.op("dve", lambda e: e.tensor_tensor(out=g3(GG), in0=g3(GG), in1=MEAN.unsqueeze(2).broadcast_to([128, 4, 128]), op=ALU.subtract),
                         reads=[bSM], writes=[bGG])
                    k.op("dve", lambda e: e.tensor_tensor(out=g3(VN), in0=g3(GG), in1=S2.unsqueeze(2).broadcast_to([128, 4, 128]), op=ALU.mult),
                         reads=[bGG, bSM], writes=[bVN])
                    k.group("pe", [lambda e, g=g: e.matmul(PS[4][:, g * 128:(g + 1) * 128], VN[:, g * 128:(g + 1) * 128], WST[:, g, :], start=True, stop=True)
                                   for g in range(4)], reads=[bVN, bC], writes=[bPS[4]])
                    for g in range(4):
                        k.op("dve", lambda e, g=g: e.scalar_tensor_tensor(out=TT[:, g * 128:(g + 1) * 128], in0=PS[4][:, g * 128:(g + 1) * 128],
                                                                          scalar=PPV[:, 16 + g:17 + g], in1=GB[:, 128 + g * 128:128 + (g + 1) * 128],
                                                                          op0=ALU.mult, op1=ALU.add),
                             reads=[bPS[4], bC], writes=[bTT])
                    k.op("dve", lambda e, gsl=gsl, tsl=tsl: e.tensor_tensor(out=GM[:, :, gsl], in0=g3(TT), in1=GU[:, :, tsl], op=ALU.mult),
                         reads=[bTT, bGU], writes=[bGM[gt]])
                    k.t = k.real
                    chains = [rq.items, rk.items, rg.items]
                    pos = [0, 0, 0]
                    while any(pos[i] < len(chains[i]) for i in range(3)):
                        for i in range(3):
                            if pos[i] < len(chains[i]):
                                kind, a_, kw_ = chains[i][pos[i]]
                                pos[i] += 1
                                getattr(k.real, kind)(*a_, **kw_)

            if STOP < 3:
                continue
            k.fence("pool")
            k.dma("pool", "ib", ib[0:128, :], KVL[:, 0, :], reads=[bKVL], writes=[bIB])
            k.dma("pool", "ib", ib[128:256, :], KVL[:, 1, :], reads=[bKVL], writes=[bIB])
            k._wait("pool", [bIB.w, bOB.w] + list(bOB.r.values()))
            tokc = k._signal("pool", nc.gpsimd.collective_compute("AllGather", ALU.bypass, replica_groups=[[0, 1], [2, 3], [4, 5], [6, 7]],
                                                                  ins=[ib.ap().opt()], outs=[ob.ap().opt()]))
            bOB.w = tokc
            bOB.r = {}
            k.barrier()
            k.fence("pool")
            for rk in range(2):
                k.dma("pool", "kt", KT[:, rk * T:(rk + 1) * T], ob[rk * 256:rk * 256 + 128, :], reads=[bOB], writes=[bKT])
                vsrc = ob[rk * 256 + 128:rk * 256 + 256, :].rearrange("p (t d) -> p t d", d=128)
                k.dma("pool", "kv", VA[:, rk * 17:(rk + 1) * 17, 0:64], vsrc[:, :, 0:64], reads=[bOB], writes=[bVA])
                k.dma("pool", "kv", VA[:, rk * 17:(rk + 1) * 17, 65:129], vsrc[:, :, 64:128], reads=[bOB], writes=[bVA])
            k.op("dve", lambda e: e.memset(VA[:, :, 64:65], 1.0), writes=[bVA])
            k.op("dve", lambda e: e.memset(VA[:, :, 129:130], 1.0), writes=[bVA])
            for s2 in range(2):
                for j in range(4):
                    r0 = (s2 * 4 + j) * 64
                    k.dma("pool", "wo", WOA[s2 * 64:(s2 + 1) * 64, j, :], w_out[l, r0:r0 + 64, :], writes=[bWB])
            for g in range(4):
                k.dma("pool", "wo", WOG[:, g, :], w_out[l, 512 + g * 128:512 + (g + 1) * 128, :], writes=[bWB])

            if STOP < 4:
                continue
            RCs, bRCs = [rstd, sq0], [b_rstd, b_sq0]
            BCSs, bBCSs = [tmp0, tmp1], [b_tmp0, b_tmp1]
            PT2 = [WA[:, u * 1024:(u + 1) * 1024].rearrange("p (a n) -> p a n", a=2) for u in range(3)]
            SP2 = [PSS[:, u * 1024:(u + 1) * 1024].rearrange("p (a n) -> p a n", a=2) for u in range(2)]

            def finalize1(bi, bs, bn, j, kvh):
                po = kvh
                RC, bRC = RCs[po], bRCs[po]
                k.op("dve", lambda e: e.reciprocal(out=RC[64:65, :bn], in_=PS[po][64:65, :bn]), reads=[bPS[po]], writes=[bRC])

            def finalize(bi, bs, bn, j, kvh):
                po = kvh
                RC, bRC, BCS, bBCS = RCs[po], bRCs[po], BCSs[po], bBCSs[po]
                k.op("pe", lambda e: e.matmul(PS[2][0:64, :bn], ONES[64:65, 0:64], RC[64:65, :bn], start=True, stop=True), reads=[bRC, bC], writes=[bPS[2]])
                k.op("dve", lambda e: e.tensor_copy(out=BCS[0:64, :bn], in_=PS[2][0:64, :bn]), reads=[bPS[2]], writes=[bBCS])
                if kvh == 0:
                    k.op("dve", lambda e: e.tensor_tensor(out=AO[0:64, j, bs:bs + bn], in0=PS[po][0:64, :bn], in1=BCS[0:64, :bn], op=ALU.mult),
                         reads=[bPS[po], bBCS], writes=[bAO[j][0][bi]])
                else:
                    stg, bstg = ([QR, VN][j % 2], [bQR, bVN][j % 2])
                    k.op("dve", lambda e: e.tensor_tensor(out=stg[0:64, :bn], in0=PS[po][0:64, :bn], in1=BCS[0:64, :bn], op=ALU.mult),
                         reads=[bPS[po], bBCS], writes=[bstg])
                    k.dma("sp", f"ao{j % 2}", AO[64:128, j, bs:bs + bn], stg[0:64, :bn], reads=[bstg], writes=[bAO[j][1][bi]])

            pending = [None]
            for bi, (bs, bn) in enumerate(BLOCKS):
                kcs = [0, 17] if bi == 0 else list(range(NKC))
                qbufs = [bQT[bs // 128 + i] for i in range(bn // 128)]
                n = len(kcs)
                for j in range(4):
                    def S2(u):
                        sp, c = u % 2, kcs[u]
                        fns = [lambda e, a=a: e.matmul(PS[3 + 2 * sp + a][:, :bn], KT[a * 64:(a + 1) * 64, c * 128:(c + 1) * 128],
                                                       QT[a * 64:(a + 1) * 64, j, bs:bs + bn], start=True, stop=True) for a in range(2)]
                        k.group("pe", fns, reads=qbufs + [bKT], writes=[bPS[3 + 2 * sp], bPS[4 + 2 * sp]])

                    def EX2(u):
                        sp, pt = u % 2, u % 3
                        k.op("act", lambda e: e.activation(out=PT2[pt][:, :, :bn], in_=SP2[sp][:, :, :bn], func=AF.Exp),
                             reads=[bPS[3 + 2 * sp], bPS[4 + 2 * sp], bWA], writes=[bPT[pt]])

                    def PV2(u):
                        pt, c = u % 3, kcs[u]
                        fns = [lambda e, a=a: e.matmul(PS[a][0:65, :bn], VA[:, c, a * 65:(a + 1) * 65], PT2[pt][:, a, :bn],
                                                       start=(u == 0), stop=(u == n - 1)) for a in range(2)]
                        k.group("pe", fns, reads=[bPT[pt], bVA], writes=[bPS[0], bPS[1]])

                    S2(0)
                    S2(1)
                    if pending[0] is not None:
                        for a in range(2):
                            finalize1(*pending[0], a)
                        for a in range(2):
                            finalize(*pending[0], a)
                        pending[0] = None
                    for u in range(n):
                        EX2(u)
                        if u + 2 < n:
                            S2(u + 2)
                        PV2(u)
                        if u == 8 and l + 1 < NL and bi >= 1:
                            slot_i = (bi - 1) * 4 + j
                            if slot_i == 0:
                                k.fence("pool")
                            if slot_i < 12:
                                mod_dma(l + 1, slot_i, WMB, bWMB, "wn")
                            if 1 <= slot_i <= 12:
                                mod_mm(slot_i - 1, WMB, bWMB, 2, [bWA])
                    pending[0] = (bi, bs, bn, j)
            for a in range(2):
                finalize1(*pending[0], a)
            for a in range(2):
                finalize(*pending[0], a)
            k.barrier()
            k._wait("pe", [(k.slots[n_][0], k.slots[n_][1], "dma") for n_ in ("ao0", "ao1") if n_ in k.slots])

            if STOP < 5:
                continue
            for bi, (bs, bn) in enumerate(BLOCKS):
                sidx = 1 if bi == 0 else 0
                gmb = [bGM[bs // 128 + i] for i in range(bn // 128)]
                for dch in range(8):
                    pb = dch % 4
                    dsl = slice(dch * 128, (dch + 1) * 128)
                    fns = [lambda e, j=j, pb=pb, dsl=dsl: e.matmul(PS[pb][:, :bn], WOA[:, j, dsl], AO[:, j, bs:bs + bn], start=(j == 0), stop=False) for j in range(4)]
                    fns += [lambda e, g=g, pb=pb, dsl=dsl: e.matmul(PS[pb][:, :bn], WOG[:, g, dsl], GM[:, g, bs:bs + bn], start=False, stop=(g == 3)) for g in range(4)]
                    k.group("pe", fns, reads=[bWB] + gmb + [bAO[j][s2][bi] for j in range(4) for s2 in range(2)], writes=[bPS[pb]])
                    k.op("dve", lambda e, dch=dch, pb=pb: e.scalar_tensor_tensor(out=XT[:, dch, bs:bs + bn], in0=PS[pb][:, :bn], scalar=MODT[:, 16 + dch, sidx:sidx + 1],
                                                                                 in1=XT[:, dch, bs:bs + bn], op0=ALU.mult, op1=ALU.add),
                         reads=[bPS[pb], bC], writes=[bXT[dch][bi]])
            k.barrier()

            if STOP < 6:
                continue
            def load_ffn(g8):
                s = g8 % 2
                for kk in range(8):
                    k.dma("pool", f"fw{s}", FW1[s][:, kk, :], w_ff1[l, kk * 128:(kk + 1) * 128, g8 * 512:(g8 + 1) * 512], writes=[bFW[s]])
                for jj in range(4):
                    k.dma("pool", f"fw{s}", FW2[s][:, jj, :], w_ff2[l, g8 * 512 + jj * 128:g8 * 512 + (jj + 1) * 128, :], writes=[bFW[s]])

            k.fence("pool")
            load_ffn(0)
            for bi, (bs, bn) in enumerate(BLOCKS):
                norm_block(bi, GE2, 24, lambda kk, bs=bs, bn=bn: H2T[:, kk, bs:bs + bn], lambda kk, bi=bi: [bH2[kk][bi]])
            Rt, bRt = [SQ, T1], [b_SQ, b_T1]
            import os
            NG = int(os.environ.get('KNG', '8'))
            units = [(g8, bi) for g8 in range(NG) for bi in range(len(BLOCKS))]

            def a_phase(ui):
                g8, bi = units[ui]
                bs, bn = BLOCKS[bi]
                s, ab = g8 % 2, ui % 2
                for jj in range(4):
                    pb = 1 + (jj % 2)
                    fns = [lambda e, kk=kk, jj=jj, pb=pb: e.matmul(PS[pb][:, :bn], FW1[s][:, kk, jj * 128:(jj + 1) * 128], H2T[:, kk, bs:bs + bn],
                                                                 start=(kk == 0), stop=(kk == 7)) for kk in range(8)]
                    k.group("pe", fns, reads=[bFW[s]] + [bH2[kk][bi] for kk in range(8)], writes=[bPS[pb]])
                    rr = jj % 2
                    k.op("act", lambda e, pb=pb, rr=rr: e.activation(out=Rt[rr][:, :bn], in_=PS[pb][:, :bn], func=AF.Relu), reads=[bPS[pb]], writes=[bRt[rr]])
                    k.op("dve", lambda e, jj=jj, rr=rr: e.tensor_tensor(out=AF2[ab][:, jj, :bn], in0=Rt[rr][:, :bn], in1=Rt[rr][:, :bn], op=ALU.mult),
                         reads=[bRt[rr]], writes=[bAF[ab]])

            def y_phase(ui):
                g8, bi = units[ui]
                bs, bn = BLOCKS[bi]
                s, ab = g8 % 2, ui % 2
                sidx = 1 if bi == 0 else 0
                for dch in range(8):
                    pb = 3 + (dch % 3)
                    dsl = slice(dch * 128, (dch + 1) * 128)
                    fns = [lambda e, jj=jj, pb=pb, dsl=dsl: e.matmul(PS[pb][:, :bn], FW2[s][:, jj, dsl], AF2[ab][:, jj, :bn], start=(jj == 0), stop=(jj == 3)) for jj in range(4)]
                    k.group("pe", fns, reads=[bFW[s], bAF[ab]], writes=[bPS[pb]])
                    k.op("dve", lambda e, dch=dch, pb=pb: e.scalar_tensor_tensor(out=XT[:, dch, bs:bs + bn], in0=PS[pb][:, :bn], scalar=MODT[:, 40 + dch, sidx:sidx + 1],
                                                                                 in1=XT[:, dch, bs:bs + bn], op0=ALU.mult, op1=ALU.add),
                         reads=[bPS[pb], bC], writes=[bXT[dch][bi]])

            if NG > 1:
                load_ffn(1)
            a_phase(0)
            for ui in range(len(units)):
                if ui + 1 < len(units):
                    a_phase(ui + 1)
                y_phase(ui)
                g8, bi = units[ui]
                if bi == len(BLOCKS) - 1:
                    if g8 + 2 < NG:
                        load_ffn(g8 + 2)
                    if g8 == 6 and NG == 8 and l + 1 < NL:
                        for kk in range(8):
                            k.dma("pool", "win", WIN[:, kk, :], w_in[l + 1, kk * 128:(kk + 1) * 128, :], writes=[bWA])
                        win_loaded.add(l + 1)
            k.barrier()

        k.fence("sp")
        OSt, bOS = [SQ, T1], [b_SQ, b_T1]
        ydone = []
        for tt in range(1, NTILE):
            bi = 1 + (tt - 1) // 4
            for hf in range(2):
                s = (2 * tt + hf) % 2
                pb = (2 * tt + hf) % 4
                fns = [lambda e, j=j, hf=hf, pb=pb: e.transpose(PS[pb][:, j * 128:(j + 1) * 128], XT[:, hf * 4 + j, tt * 128:(tt + 1) * 128], IDENT[:, :]) for j in range(4)]
                k.group("pe", fns, reads=[bXT[kk][bi] for kk in range(hf * 4, hf * 4 + 4)] + [bC], writes=[bPS[pb]])
                if hf == 0:
                    k.op("act", lambda e, s=s, pb=pb: e.activation(out=OSt[s][:, :], in_=PS[pb][:, :], func=AF.Copy), reads=[bPS[pb]], writes=[bOS[s]])
                else:
                    k.op("dve", lambda e, s=s, pb=pb: e.tensor_copy(out=OSt[s][:, :], in_=PS[pb][:, :]), reads=[bPS[pb]], writes=[bOS[s]])
                ydone.append(k.dma("sp", f"y{s}", y[(tt - 1) * 128:tt * 128, hf * 512:(hf + 1) * 512], OSt[s][:, :], reads=[bOS[s]]))
        k._wait("sp", ydone)
        k._wait("act", ydone)
    return nc


_NC_CACHE = {}


def _rope_table(half):
    tab = np.zeros((T, 96), np.float32)
    tab[:128, 0:32] = 1.0
    t = (half * 2048 + np.arange(2048)).astype(np.int64)
    row = (t // 64).astype(np.float32)
    col = (t % 64).astype(np.float32)
    inv = (np.float32(10000.0) ** (-np.arange(16, dtype=np.float32) / np.float32(16))).astype(np.float32)
    ang = np.concatenate([row[:, None] * inv[None, :], col[:, None] * inv[None, :]], axis=1).astype(np.float32)
    tab[128:, 0:32] = np.cos(ang)
    tab[128:, 32:64] = np.sin(ang)
    tab[128:, 64:96] = -np.sin(ang)
    return tab


def kernel(x, c, ctx, c_ctx, w_mod, b_mod, norm1_g, w_in, q_norm_g, k_norm_g, gmlp_norm_g,
           w_spatial, b_spatial, w_out, norm2_g, w_ff1, w_ff2, _nl=4, _stop=9):
    f = lambda a: np.ascontiguousarray(np.asarray(a, dtype=np.float32))
    x, c, ctx, c_ctx = f(x), f(c), f(ctx), f(c_ctx)
    shared = dict(w_mod=f(w_mod), b_mod=f(b_mod), norm1_g=f(norm1_g), w_in=f(w_in), q_norm_g=f(q_norm_g),
                  k_norm_g=f(k_norm_g), gmlp_norm_g=f(gmlp_norm_g), w_spatial=f(w_spatial), b_spatial=f(b_spatial),
                  w_out=f(w_out), norm2_g=f(norm2_g), w_ff1=f(w_ff1), w_ff2=f(w_ff2),
                  ident=np.eye(128, dtype=np.float32))
    if (_nl, _stop) not in _NC_CACHE:
        _NC_CACHE[(_nl, _stop)] = _build(_nl, _stop)
    nc = _NC_CACHE[(_nl, _stop)]
    in_maps = []
    for core in range(8):
        b, half = core // 2, core % 2
        xin = np.concatenate([ctx[b, half * 128:(half + 1) * 128], x[b, half * 2048:(half + 1) * 2048]], axis=0)
        m = dict(shared)
        m["xin"] = np.ascontiguousarray(xin)
        m["cvec"] = np.ascontiguousarray(np.stack([c[b], c_ctx], axis=0))
        m["rope"] = _rope_table(half)
        in_maps.append(m)
    res = run_bass_kernel_spmd(nc, in_maps, core_ids=list(range(8)))
    out = np.empty((4, 4096, D), np.float32)
    for core in range(8):
        b, half = core // 2, core % 2
        out[b, half * 2048:(half + 1) * 2048] = res.results[core]["y"]
    return out
```
